# Optimizing a Trainium2 kernel written in Bass

```python
import jax, jax.numpy as jnp
from jax import lax
import numpy as np

D_MODEL = 1024
BATCH = 4
SEQ = 8192
DEPTH = 2

N_MIXERS = 2
EXPAND = 2
D_INNER = EXPAND * D_MODEL
K_SHORT = 3
K_CONF = 31
N_META = 16
N_A = (DEPTH + 1) // 2
N_B = DEPTH // 2
RMS_EPS = 1e-6
LN_EPS = 1e-5

kernel_name = "hybrid_shortconv_conformer_trunk"


def rmsnorm(x, g):
    xf = x.astype(jnp.float32)
    y = xf * lax.rsqrt(jnp.mean(xf * xf, axis=-1, keepdims=True) + RMS_EPS)
    return (y * g.astype(jnp.float32)).astype(x.dtype)


def layernorm(x, g, b):
    xf = x.astype(jnp.float32)
    mu = jnp.mean(xf, axis=-1, keepdims=True)
    var = jnp.mean(jnp.square(xf - mu), axis=-1, keepdims=True)
    y = (xf - mu) * lax.rsqrt(var + LN_EPS)
    return (y * g.astype(jnp.float32) + b.astype(jnp.float32)).astype(x.dtype)


def causal_dwconv(u, w, b):
    K, C = w.shape
    out = lax.conv_general_dilated(
        u, w[:, None, :].astype(u.dtype),
        window_strides=(1,), padding=[(K - 1, 0)],
        dimension_numbers=("NWC", "WIO", "NWC"),
        feature_group_count=C)
    return out + b.astype(u.dtype)


def mixer_short_conv(h, w_in, conv_w, conv_b, w_out):
    proj = jnp.einsum("btd,de->bte", h, w_in)
    bg, cg, xv, z = jnp.split(proj, 4, axis=-1)
    v = bg * causal_dwconv(cg * xv, conv_w, conv_b)
    return jnp.einsum("bte,ed->btd", jax.nn.silu(z) * v, w_out)


def mixer_conformer_conv(h, w_in, conv_w, conv_b, ln_g, ln_b, w_out):
    proj = jnp.einsum("btd,de->bte", h, w_in)
    ua, ub, z = jnp.split(proj, 3, axis=-1)
    u = ua * jax.nn.sigmoid(ub)
    v = jax.nn.silu(layernorm(causal_dwconv(u, conv_w, conv_b), ln_g, ln_b))
    return jnp.einsum("bte,ed->btd", jax.nn.silu(z) * v, w_out)


def setup_inputs(seed: int = 0) -> dict:
    key = jax.random.key(seed)
    ks = jax.random.split(key, 16)
    f32 = jnp.float32
    n = lambda k, s, sc: jax.random.normal(k, s, f32) * sc
    return {
        "x": n(ks[0], (BATCH, SEQ, D_MODEL), 1.0),
        "meta": n(ks[1], (N_META, D_MODEL), 1.0),
        "norm_g": 1.0 + n(ks[2], (DEPTH, D_MODEL), 0.05),
        "a_w_in": n(ks[3], (N_A, D_MODEL, 4 * D_INNER), D_MODEL ** -0.5),
        "a_conv_w": n(ks[4], (N_A, K_SHORT, D_INNER), K_SHORT ** -0.5),
        "a_conv_b": n(ks[5], (N_A, D_INNER), 0.02),
        "a_w_out": n(ks[6], (N_A, D_INNER, D_MODEL), D_INNER ** -0.5),
        "b_w_in": n(ks[7], (N_B, D_MODEL, 3 * D_INNER), D_MODEL ** -0.5),
        "b_conv_w": n(ks[8], (N_B, K_CONF, D_INNER), K_CONF ** -0.5),
        "b_conv_b": n(ks[9], (N_B, D_INNER), 0.02),
        "b_ln_g": 1.0 + n(ks[10], (N_B, D_INNER), 0.05),
        "b_ln_b": n(ks[11], (N_B, D_INNER), 0.02),
        "b_w_out": n(ks[12], (N_B, D_INNER, D_MODEL), D_INNER ** -0.5),
        "final_g": 1.0 + n(ks[13], (D_MODEL,), 0.05),
    }


def reference(x, meta, norm_g, a_w_in, a_conv_w, a_conv_b, a_w_out,
              b_w_in, b_conv_w, b_conv_b, b_ln_g, b_ln_b, b_w_out, final_g):
    bsz = x.shape[0]
    meta_b = jnp.broadcast_to(meta.astype(x.dtype)[None], (bsz, N_META, D_MODEL))
    h = jnp.concatenate([meta_b, x], axis=1)
    for i in range(DEPTH):
        hn = rmsnorm(h, norm_g[i])
        j = i // N_MIXERS
        if i % N_MIXERS == 0:
            y = mixer_short_conv(hn, a_w_in[j], a_conv_w[j], a_conv_b[j], a_w_out[j])
        else:
            y = mixer_conformer_conv(hn, b_w_in[j], b_conv_w[j], b_conv_b[j],
                                     b_ln_g[j], b_ln_b[j], b_w_out[j])
        h = h + y
    return rmsnorm(h, final_g)[:, N_META:]
```

```python
import numpy as np
from contextlib import ExitStack

import concourse.bass as bass
import concourse.mybir as mybir
from concourse.bass_utils import run_bass_kernel_spmd

F32 = mybir.dt.float32
BF16 = mybir.dt.bfloat16
I32 = mybir.dt.int32
AF = mybir.ActivationFunctionType
ALU = mybir.AluOpType

D = 1024
DI = 2048
SEQ = 8192
BATCH = 4
NMETA = 16
NCORES = 8
N = 412
NT = 10
TOK = N * NT
HALO = 32
KC = D // 128
JC = DI // 128
NG = 36
GW = 4096
RMS_EPS = 1e-6
LN_EPS = 1e-5
NPAR = 38
KP = 23
NRING = 5
BLK = [(0, 128), (128, 128), (256, 128), (384, N - 384)]

ENGS = ("pe", "act", "dve", "pool", "sp")


class Buf:
    __slots__ = ("name", "lw", "lr")

    def __init__(self, name):
        self.name = name
        self.lw = {}
        self.lr = {}


class Op:
    __slots__ = ("id", "eng", "fn", "waits", "signal", "sigval", "dma_sem", "dma_val", "dma_key")

    def __init__(self, id, eng, fn):
        self.id = id
        self.eng = eng
        self.fn = fn
        self.waits = []
        self.signal = False
        self.sigval = None
        self.dma_sem = None
        self.dma_val = None
        self.dma_key = None


class Prog:
    def __init__(self):
        self.ops = []
        self.waited = {e: {} for e in ENGS}
        self.dma_cnt = {}

    def add(self, eng, fn, reads=(), writes=(), dma=None):
        op = Op(len(self.ops), eng, fn)
        raw = set()
        other = set()
        for b in reads:
            raw.update(b.lw.values())
        for b in writes:
            other.update(b.lw.values())
            other.update(b.lr.values())
        deps = {}
        for d in raw:
            deps[d] = True
        for d in other:
            deps.setdefault(d, False)
        for d in sorted(deps):
            is_raw = deps[d]
            dop = self.ops[d]
            if dop.dma_sem is not None:
                key = ("dma", dop.dma_key)
                if self.waited[eng].get(key, 0) >= dop.dma_val:
                    continue
                self.waited[eng][key] = dop.dma_val
                op.waits.append(("dma", dop.dma_sem, dop.dma_val))
            else:
                p = dop.eng
                if p == eng and dma is None:
                    if eng == "pe":
                        continue
                key = ("eng", p)
                if self.waited[eng].get(key, -1) >= d:
                    continue
                self.waited[eng][key] = d
                dop.signal = True
                op.waits.append(("eng", p, d))
        if dma is not None:
            name, sem = dma
            self.dma_cnt[name] = self.dma_cnt.get(name, 0) + 16
            op.dma_sem = sem
            op.dma_val = self.dma_cnt[name]
            op.dma_key = name
            k = ("dma", name)
        else:
            k = eng
        for b in reads:
            b.lr[k] = op.id
        for b in writes:
            b.lw[k] = op.id
        self.ops.append(op)
        return op

    def emit(self, engobj, sems):
        cnt = {e: 0 for e in ENGS}
        for op in self.ops:
            eng = engobj[op.eng]
            for w in op.waits:
                if w[0] == "eng":
                    eng.wait_ge(sems[w[1]], self.ops[w[2]].sigval)
                else:
                    eng.wait_ge(w[1], w[2])
            ins = op.fn(eng)
            if op.dma_sem is not None:
                ins.then_inc(op.dma_sem, 16)
            elif op.signal:
                cnt[op.eng] += 1
                op.sigval = cnt[op.eng]
                ins.then_inc(sems[op.eng], 1)


class Rot:
    def __init__(self, items):
        self.items = items
        self.i = 0

    def next(self):
        it = self.items[self.i % len(self.items)]
        self.i += 1
        return it


def build_nc(nt=NT, flags=()):
    nc = bass.Bass("TRN2", target_bir_lowering=False)
    xin = nc.dram_tensor("xin", [TOK, D], F32, kind="ExternalInput").ap()
    wpack = nc.dram_tensor("wpack", [NG, 128, GW], F32, kind="ExternalInput").ap()
    chanp_d = nc.dram_tensor("chanp", [128, JC * NPAR], F32, kind="ExternalInput").ap()
    dmp_d = nc.dram_tensor("dmp", [128, KC * 2], F32, kind="ExternalInput").ap()
    gfin_d = nc.dram_tensor("gfin", [128, D], F32, kind="ExternalInput").ap()
    wbf = nc.dram_tensor("wbf", [NG, 128, GW], BF16).ap()
    dgb = nc.dram_tensor("dgb", [JC, 128, KP * 128], BF16).ap()
    out = nc.dram_tensor("out", [TOK, D], F32, kind="ExternalOutput").ap()

    with ExitStack() as es:
        E = es.enter_context
        sb = lambda name, shape, dt: E(nc.sbuf_tensor(name, shape, dt))
        H = sb("H", [128, KC, N], F32)
        HN = sb("HN", [128, KC, N], BF16)
        PC = sb("PC", [128, JC, N + 2], F32)
        G = sb("G", [128, JC, N], BF16)
        U = sb("U", [128, JC, N + 30], BF16)
        DIAG = sb("DIAG", [128, 1, 31, 128], BF16)
        SQ = sb("SQ", [128, KC, N], BF16)
        SZ16 = sb("SZ16", [128, JC, N], BF16)
        WR = sb("WR", [128, NRING, GW], BF16)
        XS = sb("XS", [128, 4, D], F32)
        OS = sb("OS", [128, 2, D], F32)
        JUNK = sb("JUNK", [128, D], BF16)
        TMP = sb("TMP", [128, 14, N], F32)
        CB = sb("CB", [128, 4, N], BF16)
        ST = sb("ST", [128, 5, N], F32)
        CST = sb("CST", [128, 1], F32)
        SM = sb("SM", [128, 8], F32)
        EPS_R = sb("EPS_R", [128, 1], F32)
        identf = sb("identf", [128, 128], F32)
        identb = sb("identb", [128, 128], BF16)
        ones_dm = sb("ones_dm", [128, 128], BF16)
        ones_ch = sb("ones_ch", [128, 128], BF16)
        idx = sb("idx", [128, 128], I32)
        CHP = sb("CHP", [128, JC, NPAR], F32)
        DMP = sb("DMP", [128, KC, 2], F32)
        GFIN = sb("GFIN", [128, D], F32)
        Dp = [E(nc.psum_tensor(f"D{i}", [128, 1024], F32)) for i in range(4)]
        pbig = Dp[3]
        banks = [Dp[i // 2][:, (i % 2) * 512:(i % 2) * 512 + 512] for i in range(6)]

        sems = {e: E(nc.semaphore(f"s_{e}")) for e in ENGS}
        dsem = {}

        def dma_sem(name):
            if name not in dsem:
                dsem[name] = E(nc.semaphore(f"d_{name}"))
            return (name, dsem[name])

        P = Prog()

        Hb = [Buf(f"H{k}") for k in range(KC)]
        HNb = [Buf(f"HN{k}") for k in range(KC)]
        PCbody = [Buf(f"PCb{j}") for j in range(JC)]
        PChalo = [Buf(f"PCh{j}") for j in range(JC)]
        Gb = [Buf(f"G{j}") for j in range(JC)]
        Ubody = [Buf(f"Ub{j}") for j in range(JC)]
        Uhalo = [Buf(f"Uh{j}") for j in range(JC)]
        DIAGb = [Buf("DG0")]
        SQb = [Buf(f"SQ{k}") for k in range(KC)]
        SZb = [Buf(f"SZ{j}") for j in range(JC)]
        WRb = [Buf(f"WR{s}") for s in range(NRING)]
        DGBb = [Buf(f"DGB{j}") for j in range(JC)]
        WBFb = [Buf(f"WBF{g}") for g in range(NG)]
        XSb = Buf("XS")
        OSb = [Buf("OS0"), Buf("OS1")]
        JUNKb = Buf("JUNK")
        S0b = Buf("S0")
        S1b = Buf("S1")
        bankb = [Buf(f"bank{i}") for i in range(6)]
        CONSTb = Buf("const")
        ring = Rot([(banks[i], bankb[i]) for i in range(6)])
        tmpb = [Buf(f"TMP{i}") for i in range(14)]
        ACCr = Rot([(TMP[:, 10 + i, :], tmpb[10 + i]) for i in range(4)])
        XVr = Rot([(TMP[:, i, :], tmpb[i]) for i in (0, 1, 8, 9)])
        SZr = Rot([(TMP[:, 2, :], tmpb[2]), (TMP[:, 3, :], tmpb[3])])
        Ar = Rot([(TMP[:, 4, :], tmpb[4]), (TMP[:, 5, :], tmpb[5])])
        Vr = Rot([(TMP[:, 6, :], tmpb[6]), (TMP[:, 7, :], tmpb[7])])
        SGr = Rot([(TMP[:, 8, :], tmpb[8]), (TMP[:, 9, :], tmpb[9])])
        cbb = [Buf(f"CB{i}") for i in range(4)]
        CBFr = Rot([(CB[:, 0, :], cbb[0]), (CB[:, 1, :], cbb[1])])
        CSQr = Rot([(CB[:, 2, :], cbb[2]), (CB[:, 3, :], cbb[3])])
        stb = [Buf(f"ST{i}") for i in range(5)]
        MS, RSTD, MEAN, M2, MR = [(ST[:, i, :], stb[i]) for i in range(5)]
        smb = [Buf(f"SM{i}") for i in range(8)]
        SSQr = Rot([(SM[:, 0:1], smb[0]), (SM[:, 1:2], smb[1])])
        MS1r = Rot([(SM[:, 2:3], smb[2]), (SM[:, 3:4], smb[3])])
        RS1r = Rot([(SM[:, 4:5], smb[4]), (SM[:, 5:6], smb[5])])
        OSr = Rot([(0, OSb[0]), (1, OSb[1])])
        S0 = pbig[:, 0:N]
        S1 = pbig[:, 512:512 + N]

        P.add("pool", lambda e: e.iota(idx[:], pattern=[[1, 128]], base=0, channel_multiplier=-1), writes=[CONSTb])
        P.add("dve", lambda e: e.tensor_scalar(out=identf[:], in0=idx[:], scalar1=0.0, scalar2=None, op0=ALU.is_equal),
              reads=[CONSTb], writes=[CONSTb])
        P.add("dve", lambda e: e.tensor_copy(out=identb[:], in_=identf[:]), reads=[CONSTb], writes=[CONSTb])
        P.add("dve", lambda e: e.memset(ones_dm[:], 1.0 / D), writes=[CONSTb])
        P.add("dve", lambda e: e.memset(ones_ch[:], 1.0 / DI), writes=[CONSTb])
        P.add("dve", lambda e: e.memset(CST[:], -0.5), writes=[CONSTb])
        P.add("dve", lambda e: e.memset(EPS_R[:], RMS_EPS), writes=[CONSTb])
        P.add("dve", lambda e: e.memset(PC[:, :, 0:2], 0.0), writes=PChalo)
        P.add("dve", lambda e: e.memset(U[:, :, 0:30], 0.0), writes=Uhalo)
        P.add("sp", lambda e: e.dma_start(out=CHP[:], in_=chanp_d.rearrange("p (j c) -> p j c", c=NPAR)),
              writes=[CONSTb], dma=dma_sem("par0"))
        P.add("sp", lambda e: e.dma_start(out=DMP[:], in_=dmp_d.rearrange("p (k c) -> p k c", c=2)),
              writes=[CONSTb], dma=dma_sem("par1"))
        P.add("sp", lambda e: e.dma_start(out=GFIN[:], in_=gfin_d[:, :]), writes=[CONSTb], dma=dma_sem("par2"))

        seq = [("w", g) for g in range(20)]
        for jj in range(JC // 2):
            seq.append(("w", 20 + jj))
            if jj >= 1:
                seq.append(("dg", 2 * jj - 1))
            seq.append(("dg", 2 * jj))
        seq.append(("dg", JC - 1))
        seq += [("w", g) for g in range(28, 36)]
        NSEQ = len(seq)
        state = {"cast": 0, "load": 0, "use": 0}
        loaded_q = {}
        cur_q = {}
        total_groups = nt * NSEQ

        widx = {}
        for kind_, g_ in seq:
            if kind_ == "w":
                widx.setdefault(g_, len(widx))

        def issue_load():
            q = state["load"]
            if q >= total_groups:
                return
            state["load"] += 1
            kind, g = seq[q % NSEQ]
            s = q % NRING
            loaded_q[s] = q
            if kind == "w":
                dfr = widx[g] % 2 == 1
                if q < NSEQ:
                    if dfr and state.get("xs_free", False):
                        P.add("sp", lambda e, g=g: e.dma_start(out=XS[:].rearrange("p b d -> p (b d)"), in_=wpack[g]),
                              writes=[XSb], dma=dma_sem("stg"))

                        def castop(g=g, s=s):
                            P.add("act", lambda e: e.activation(out=WR[:, s, :], in_=XS[:].rearrange("p b d -> p (b d)"),
                                                                func=AF.Copy), reads=[XSb], writes=[WRb[s]])
                        pend_cast.append(castop)
                    else:
                        P.add("pool", lambda e, g=g, s=s: e.dma_start(out=WR[:, s, :], in_=wpack[g]),
                              writes=[WRb[s]], dma=dma_sem(f"wrs{s}"))
                        if not dfr:
                            pend_store[q] = (g, s)
                elif q < 2 * NSEQ and dfr:
                    P.add("pool", lambda e, g=g, s=s: e.dma_start(out=WR[:, s, :], in_=wpack[g]),
                          writes=[WRb[s]], dma=dma_sem(f"wrs{s}"))
                    pend_store[q] = (g, s)
                else:
                    assert WBFb[g].lw, ("bf16 copy not stored yet", g, q)
                    P.add("sp", lambda e, g=g, s=s: e.dma_start(out=WR[:, s, :], in_=wbf[g]),
                          reads=[WBFb[g]], writes=[WRb[s]], dma=dma_sem(f"wr{s}"))
            else:
                assert DGBb[g].lw, "diag group loaded before it was built"
                P.add("sp", lambda e, g=g, s=s: e.dma_start(out=WR[:, s, 0:KP * 128], in_=dgb[g]),
                      reads=[DGBb[g]], writes=[WRb[s]], dma=dma_sem(f"wr{s}"))

        pend_cast = []
        pend_store = {}

        def flush_stores(upto):
            for q0 in sorted(pend_store):
                if q0 < upto:
                    g0, s0 = pend_store.pop(q0)
                    P.add("sp", lambda e, g0=g0, s0=s0: e.dma_start(out=wbf[g0], in_=WR[:, s0, :]),
                          reads=[WRb[s0]], writes=[WBFb[g0]], dma=dma_sem(f"wst{s0}"))

        def next_group(kind, g, defer=False):
            while pend_cast:
                pend_cast.pop(0)()
            q = state["use"]
            flush_stores(q)
            assert seq[q % NSEQ] == (kind, g), (q, seq[q % NSEQ], kind, g)
            state["use"] += 1
            if not defer:
                issue_load()
            cur_q[q % NRING] = q
            return q % NRING

        for _ in range(NRING - 1):
            issue_load()

        def build_diag(j):
            ds = 0

            def f(e):
                for k in range(KP):
                    ins = e.tensor_scalar(out=DIAG[:, ds, k, :], in0=identb[:], scalar1=CHP[:, j, 4 + k:5 + k],
                                          scalar2=None, op0=ALU.mult)
                return ins
            P.add("dve", f, reads=[CONSTb], writes=[DIAGb[ds]])
            P.add("sp", lambda e: e.dma_start(out=dgb[j], in_=DIAG[:, ds, 0:KP, :].rearrange("p k m -> p (k m)")),
                  reads=[DIAGb[ds]], writes=[DGBb[j]], dma=dma_sem(f"dgst{ds}"))

        def wsl(s, i):
            return WR[:, s, i * 128:(i + 1) * 128]

        def stats_act(kc, src_ap, src_bufs):
            P.add("act", lambda e: e.activation(out=SQ[:, kc, :], in_=src_ap, func=AF.Square),
                  reads=src_bufs, writes=[SQb[kc]])

        def stats_mm(kc):
            P.add("pe", lambda e: e.matmul(S0, lhsT=ones_dm[:], rhs=SQ[:, kc, :], start=(kc == 0), stop=(kc == KC - 1)),
                  reads=[SQb[kc], CONSTb], writes=[S0b])

        def preload_ln():
            P.add("act", lambda e: e.activation(out=SM[:, 6:7], in_=EPS_R[:, 0:1], func=AF.Ln),
                  reads=[CONSTb], writes=[smb[6]])

        def rms_finish(layer):
            P.add("act", lambda e: e.activation(out=MS[0], in_=S0, func=AF.Ln, bias=EPS_R[:, 0:1], scale=1.0),
                  reads=[S0b, CONSTb], writes=[MS[1]])
            P.add("act", lambda e: e.activation(out=RSTD[0], in_=MS[0], func=AF.Exp, scale=-0.5),
                  reads=[MS[1]], writes=[RSTD[1]])
            for kc in range(KC):
                P.add("dve", lambda e, kc=kc: e.scalar_tensor_tensor(
                    out=HN[:, kc, :], in0=H[:, kc, :], scalar=DMP[:, kc, layer:layer + 1], in1=RSTD[0],
                    op0=ALU.mult, op1=ALU.mult), reads=[Hb[kc], RSTD[1], CONSTb], writes=[HNb[kc]])

        def chk(slot):
            assert loaded_q[slot] == cur_q[slot], (slot, loaded_q[slot], cur_q[slot])

        def proj(slot, wi0, rhs_t, nk, reads):
            chk(slot)
            bank, bb = ring.next()

            def f(e):
                for kc in range(nk):
                    ins = e.matmul(bank[:, 0:N], lhsT=wsl(slot, wi0 + kc), rhs=rhs_t[:, kc, :],
                                   start=(kc == 0), stop=(kc == nk - 1))
                return ins
            P.add("pe", f, reads=[WRb[slot]] + reads, writes=[bb])
            return bank[:, 0:N], bb

        def proj_multi(slot, wi0s, rhs_t, nk, rbufs):
            chk(slot)
            outs = [ring.next() for _ in wi0s]
            for kc in range(nk):
                def f(e, kc=kc):
                    for (bank, bb), wi0 in zip(outs, wi0s):
                        ins = e.matmul(bank[:, 0:N], lhsT=wsl(slot, wi0 + kc), rhs=rhs_t[:, kc, :],
                                       start=(kc == 0), stop=(kc == nk - 1))
                    return ins
                P.add("pe", f, reads=[WRb[slot], rbufs[kc]], writes=[bb for _, bb in outs])
            return [(bank[:, 0:N], bb) for bank, bb in outs]

        def out_proj_m(wg0):
            pend = None
            for m in range(KC):
                if m % 2 == 0:
                    slot = next_group("w", wg0 + m // 2)
                ml = m % 2
                ps, pb = proj(slot, ml * JC, G, JC, Gb)
                if pend is not None:
                    stats_mm(pend)
                P.add("dve", lambda e, m=m, ps=ps: e.tensor_tensor(out=H[:, m, :], in0=H[:, m, :], in1=ps, op=ALU.add),
                      reads=[Hb[m], pb], writes=[Hb[m]])
                stats_act(m, H[:, m, :], [Hb[m]])
                pend = m
            stats_mm(pend)

        def load_x(t):
            if t >= nt:
                return
            tk = t * N
            P.add("sp", lambda e: e.dma_start(out=XS[:, 0:3, :],
                                              in_=xin[tk:tk + 384, :].rearrange("(b p) d -> p b d", p=128)),
                  writes=[XSb], dma=dma_sem("xsa"))
            P.add("sp", lambda e: e.dma_start(out=XS[0:N - 384, 3, :], in_=xin[tk + 384:tk + N, :]),
                  writes=[XSb], dma=dma_sem("xsb"))

        def tile(t):
            tok0 = t * N
            if t == 0:
                load_x(0)
            pend = None
            for kc in range(KC):
                bank, bb = ring.next()

                def f(e, kc=kc, bank=bank):
                    for b, (o, nb) in enumerate(BLK):
                        ins = e.transpose(out=bank[:, o:o + nb], in_=XS[0:nb, b, kc * 128:(kc + 1) * 128],
                                          identity=identf[0:nb, 0:nb])
                    return ins
                P.add("pe", f, reads=[XSb, CONSTb], writes=[bb])
                if pend is not None:
                    stats_mm(pend)
                P.add("act", lambda e, kc=kc, bank=bank: e.activation(out=H[:, kc, :], in_=bank[:, 0:N], func=AF.Copy),
                      reads=[bb], writes=[Hb[kc]])
                stats_act(kc, bank[:, 0:N], [bb])
                pend = kc
            stats_mm(pend)
            if t > 0:
                load_x(t + 1)
            else:
                state["xs_free"] = True

            rms_finish(0)
            for j in range(JC):
                slot = next_group("w", j)
                if t == 0:
                    build_diag(j)
                if j == 0:
                    (xv_ps, xv_b), (cg_ps, cg_b), (z_ps, z_b), (bg_ps, bg_b) = proj_multi(
                        slot, [2 * KC, 1 * KC, 3 * KC, 0 * KC], HN, KC, HNb)
                else:
                    xv_ps, xv_b = proj(slot, 2 * KC, HN, KC, HNb)
                    cg_ps, cg_b = proj(slot, 1 * KC, HN, KC, HNb)
                    z_ps, z_b = proj(slot, 3 * KC, HN, KC, HNb)
                    bg_ps, bg_b = proj(slot, 0 * KC, HN, KC, HNb)
                xv, xvb = XVr.next()
                sz, szb = SZr.next()
                a, ab = Ar.next()
                v, vb = Vr.next()
                P.add("act", lambda e, xv=xv, xv_ps=xv_ps: e.activation(out=xv, in_=xv_ps, func=AF.Copy),
                      reads=[xv_b], writes=[xvb])
                P.add("act", lambda e, sz=sz, z_ps=z_ps: e.activation(out=sz, in_=z_ps, func=AF.Silu),
                      reads=[z_b], writes=[szb])
                P.add("dve", lambda e, j=j, cg_ps=cg_ps, xv=xv: e.tensor_tensor(
                    out=PC[:, j, 2:N + 2], in0=cg_ps, in1=xv, op=ALU.mult),
                    reads=[cg_b, xvb], writes=[PCbody[j]])
                P.add("dve", lambda e, j=j, a=a: e.tensor_scalar(
                    out=a, in0=PC[:, j, 0:N], scalar1=CHP[:, j, 0:1], scalar2=CHP[:, j, 3:4],
                    op0=ALU.mult, op1=ALU.add), reads=[PCbody[j], PChalo[j], CONSTb], writes=[ab])
                for k in (1, 2):
                    P.add("dve", lambda e, j=j, a=a, k=k: e.scalar_tensor_tensor(
                        out=a, in0=PC[:, j, k:N + k], scalar=CHP[:, j, k:k + 1], in1=a,
                        op0=ALU.mult, op1=ALU.add), reads=[PCbody[j], PChalo[j], ab, CONSTb], writes=[ab])
                P.add("dve", lambda e, v=v, bg_ps=bg_ps, a=a: e.tensor_tensor(out=v, in0=bg_ps, in1=a, op=ALU.mult),
                      reads=[bg_b, ab], writes=[vb])
                P.add("dve", lambda e, j=j, v=v, sz=sz: e.tensor_tensor(out=G[:, j, :], in0=v, in1=sz, op=ALU.mult),
                      reads=[vb, szb], writes=[Gb[j]])
                P.add("act", lambda e, j=j: e.activation(out=PC[:, j, 0:2], in_=PC[:, j, N:N + 2], func=AF.Copy),
                      reads=[PCbody[j]], writes=[PChalo[j]])
            preload_ln()
            out_proj_m(16)

            rms_finish(1)
            pend_stats = []

            def conv_pe(j):
                dslot = next_group("dg", j)
                chk(dslot)
                bank, bb = ring.next()

                def f(e):
                    for k in range(KP):
                        ins = e.matmul(bank[:, 0:N], lhsT=wsl(dslot, k), rhs=U[:, j, k:k + N],
                                       start=(k == 0), stop=(k == KP - 1))
                    return ins
                P.add("pe", f, reads=[Ubody[j], Uhalo[j], WRb[dslot]], writes=[bb])
                return [j, bank[:, 0:N], bb]

            def conv_tap(c, k):
                j, src, srcb, acc, accb = c
                P.add("dve", lambda e: e.scalar_tensor_tensor(
                    out=acc, in0=U[:, j, k:k + N], scalar=CHP[:, j, 4 + k:5 + k], in1=src,
                    op0=ALU.mult, op1=ALU.add), reads=[Ubody[j], Uhalo[j], srcb, CONSTb], writes=[accb])
                c[1], c[2] = acc, accb

            def conv_first(ctxs):
                if KP < 31:
                    for c in ctxs:
                        c.extend(ACCr.next())
                        conv_tap(c, KP)

            def conv_post(ctxs):
                while pend_stats:
                    pend_stats.pop(0)()
                for k in range(KP + 1, 31):
                    for c in ctxs:
                        conv_tap(c, k)
                ctxs = [c[0:3] for c in ctxs]
                for j, src, srcb in ctxs:
                    conv_evac(j, src, srcb)

            def conv_evac(j, src, srcb):
                cbf, cbfb = CBFr.next()
                csq, csqb = CSQr.next()
                P.add("act", lambda e: e.activation(out=PC[:, j, 2:N + 2], in_=src, func=AF.Identity,
                                                    bias=CHP[:, j, 35:36], scale=0.5),
                      reads=[srcb, CONSTb], writes=[PCbody[j]])
                P.add("act", lambda e: e.activation(out=csq, in_=src, func=AF.Square,
                                                    bias=CHP[:, j, 35:36], scale=0.5),
                      reads=[srcb, CONSTb], writes=[csqb])
                P.add("act", lambda e: e.activation(out=cbf, in_=src, func=AF.Identity,
                                                    bias=CHP[:, j, 35:36], scale=0.5),
                      reads=[srcb, CONSTb], writes=[cbfb])

                def g(e):
                    e.matmul(S0, lhsT=ones_ch[:], rhs=cbf, start=(j == 0), stop=(j == JC - 1))
                    return e.matmul(S1, lhsT=ones_ch[:], rhs=csq, start=(j == 0), stop=(j == JC - 1))
                pend_stats.append(lambda: P.add("pe", g, reads=[cbfb, csqb, CONSTb], writes=[S0b, S1b]))
                P.add("act", lambda e: e.activation(out=U[:, j, 0:30], in_=U[:, j, N:N + 30], func=AF.Copy),
                      reads=[Ubody[j]], writes=[Uhalo[j]])

            def phaseA(j, slot, pre=None):
                jl = j % 2
                if pre is not None:
                    (ua_ps, ua_b), (ub_ps, ub_b) = pre
                else:
                    ua_ps, ua_b = proj(slot, (jl * 2 + 0) * KC, HN, KC, HNb)
                    ub_ps, ub_b = proj(slot, (jl * 2 + 1) * KC, HN, KC, HNb)
                sg, sgb = SGr.next()
                P.add("act", lambda e: e.activation(out=sg, in_=ub_ps, func=AF.Tanh, scale=0.5),
                      reads=[ub_b], writes=[sgb])
                P.add("dve", lambda e: e.scalar_tensor_tensor(out=U[:, j, 30:N + 30], in0=sg, scalar=1.0, in1=ua_ps,
                                                              op0=ALU.add, op1=ALU.mult),
                      reads=[ua_b, sgb], writes=[Ubody[j]])

            def zproj_pe(j, slot):
                return (j,) + proj(slot, (j % 4) * KC, HN, KC, HNb)

            def zproj_act(j, z_ps, z_b):
                P.add("act", lambda e: e.activation(out=SZ16[:, j, :], in_=z_ps, func=AF.Silu),
                      reads=[z_b], writes=[SZb[j]])

            def zproj(j, slot):
                zproj_act(*zproj_pe(j, slot))

            pend_post = None
            for jj in range(JC // 2):
                slot = next_group("w", 20 + jj)
                if jj == 0:
                    r4 = proj_multi(slot, [0, KC, 2 * KC, 3 * KC], HN, KC, HNb)
                    phaseA(0, slot, pre=r4[0:2])
                    phaseA(1, slot, pre=r4[2:4])
                else:
                    phaseA(2 * jj, slot)
                    phaseA(2 * jj + 1, slot)
                if pend_post is not None:
                    conv_post(pend_post)
                cs = []
                if jj >= 1:
                    cs.append(conv_pe(2 * jj - 1))
                cs.append(conv_pe(2 * jj))
                conv_first(cs)
                pend_post = cs
            conv_post(pend_post)
            cs = [conv_pe(JC - 1)]
            conv_first(cs)
            preload_ln()
            ZLEAD = 5
            zlead = []
            for j in range(ZLEAD):
                if j % 4 == 0:
                    zslot = next_group("w", 28 + j // 4)
                zlead.append(zproj_pe(j, zslot))
                if j == 2:
                    conv_post(cs)
            while pend_stats:
                pend_stats.pop(0)()
            P.add("dve", lambda e: e.tensor_copy(out=MEAN[0], in_=S0), reads=[S0b], writes=[MEAN[1]])
            P.add("dve", lambda e: e.tensor_tensor(out=M2[0], in0=MEAN[0], in1=MEAN[0], op=ALU.mult),
                  reads=[MEAN[1]], writes=[M2[1]])
            P.add("dve", lambda e: e.scalar_tensor_tensor(out=MS[0], in0=S1, scalar=LN_EPS, in1=M2[0],
                                                          op0=ALU.add, op1=ALU.subtract),
                  reads=[S1b, M2[1]], writes=[MS[1]])
            P.add("act", lambda e: e.activation(out=M2[0], in_=MS[0], func=AF.Ln), reads=[MS[1]], writes=[M2[1]])
            P.add("act", lambda e: e.activation(out=RSTD[0], in_=M2[0], func=AF.Exp, scale=-0.5),
                  reads=[M2[1]], writes=[RSTD[1]])
            for zl in zlead:
                zproj_act(*zl)
            P.add("dve", lambda e: e.tensor_tensor(out=MR[0], in0=MEAN[0], in1=RSTD[0], op=ALU.mult),
                  reads=[MEAN[1], RSTD[1]], writes=[MR[1]])

            def norm_a(j):
                a, ab = Ar.next()
                v, vb = Vr.next()
                xv, xvb = XVr.next()
                P.add("dve", lambda e: e.tensor_tensor(out=a, in0=PC[:, j, 2:N + 2], in1=RSTD[0], op=ALU.mult),
                      reads=[PCbody[j], RSTD[1]], writes=[ab])
                P.add("dve", lambda e: e.tensor_tensor(out=v, in0=a, in1=MR[0], op=ALU.subtract),
                      reads=[ab, MR[1]], writes=[vb])
                P.add("act", lambda e: e.activation(out=xv, in_=v, func=AF.Silu,
                                                    bias=CHP[:, j, 37:38], scale=CHP[:, j, 36:37]),
                      reads=[vb, CONSTb], writes=[xvb])
                return xv, xvb

            def norm_g(j, xv, xvb):
                P.add("dve", lambda e: e.tensor_tensor(out=G[:, j, :], in0=xv, in1=SZ16[:, j, :], op=ALU.mult),
                      reads=[xvb, SZb[j]], writes=[Gb[j]])

            SKEW = 2
            pendn = []
            for j in range(JC):
                jz = j + ZLEAD
                if jz < JC:
                    if jz % 4 == 0:
                        zslot = next_group("w", 28 + jz // 4)
                    zproj(jz, zslot)
                pendn.append((j,) + norm_a(j))
                if len(pendn) > SKEW:
                    norm_g(*pendn.pop(0))
            while pendn:
                norm_g(*pendn.pop(0))

            oslots = [next_group("w", 32, defer=False), next_group("w", 33, defer=True)]
            for sl in oslots:
                chk(sl)
            accs = [ring.next() for _ in range(4)]
            for j in range(JC):
                def f(e, j=j):
                    for m in range(4):
                        ins = e.matmul(accs[m][0][:, 0:N], lhsT=wsl(oslots[m // 2], (m % 2) * JC + j), rhs=G[:, j, :],
                                       start=(j == 0), stop=(j == JC - 1))
                    return ins
                P.add("pe", f, reads=[Gb[j]] + [WRb[sl] for sl in oslots], writes=[ab_ for _, ab_ in accs])
            issue_load()
            for m in range(4):
                P.add("dve", lambda e, m=m: e.tensor_tensor(out=H[:, m, :], in0=H[:, m, :], in1=accs[m][0][:, 0:N], op=ALU.add),
                      reads=[Hb[m], accs[m][1]], writes=[Hb[m]])
            for m in range(4, KC):
                if m % 2 == 0:
                    slot = next_group("w", 32 + m // 2)
                ps, pb = proj(slot, (m % 2) * JC, G, JC, Gb)
                P.add("dve", lambda e, m=m, ps=ps: e.tensor_tensor(out=H[:, m, :], in0=H[:, m, :], in1=ps, op=ALU.add),
                      reads=[Hb[m], pb], writes=[Hb[m]])

            preload_ln()
            dbl = [(Dp[3], [S0b, S1b]), (Dp[2], [bankb[4], bankb[5]])]
            for b, (o, nb) in enumerate(BLK):
                pt, ptb = dbl[b % 2]

                def f(e, o=o, nb=nb, pt=pt):
                    for kc in range(KC):
                        ins = e.transpose(out=pt[0:nb, kc * 128:(kc + 1) * 128], in_=H[:, kc, o:o + nb],
                                          identity=identf[:, :])
                    return ins
                P.add("pe", f, reads=Hb + [CONSTb], writes=ptb)
                ssq, ssqb = SSQr.next()
                ms1, ms1b = MS1r.next()
                rs1, rs1b = RS1r.next()
                osl, osb = OSr.next()
                P.add("act", lambda e, nb=nb, ssq=ssq, pt=pt: e.activation(out=JUNK[0:nb, :], in_=pt[0:nb, :], func=AF.Square,
                                                                         accum_out=ssq[0:nb, :]),
                      reads=ptb, writes=[JUNKb, ssqb])
                P.add("act", lambda e, nb=nb, ssq=ssq, ms1=ms1: e.activation(
                    out=ms1[0:nb, :], in_=ssq[0:nb, :], func=AF.Ln, bias=EPS_R[0:nb, 0:1], scale=1.0 / D),
                    reads=[ssqb, CONSTb], writes=[ms1b])
                P.add("act", lambda e, nb=nb, ms1=ms1, rs1=rs1: e.activation(
                    out=rs1[0:nb, :], in_=ms1[0:nb, :], func=AF.Exp, scale=-0.5),
                    reads=[ms1b], writes=[rs1b])
                P.add("dve", lambda e, nb=nb, rs1=rs1, osl=osl, pt=pt: e.scalar_tensor_tensor(
                    out=OS[0:nb, osl, :], in0=pt[0:nb, :], scalar=rs1[0:nb, :], in1=GFIN[0:nb, :],
                    op0=ALU.mult, op1=ALU.mult), reads=ptb + [rs1b, CONSTb], writes=[osb])
                P.add("sp", lambda e, nb=nb, o=o, osl=osl: e.dma_start(out=out[tok0 + o:tok0 + o + nb, :],
                                                                     in_=OS[0:nb, osl, :]),
                      reads=[osb], dma=dma_sem(f"os{osl}"))

        for t in range(nt):
            tile(t)
            if t == 0:
                while pend_cast:
                    pend_cast.pop(0)()
                load_x(1)

        engobj = {"pe": nc.tensor, "act": nc.scalar, "dve": nc.vector, "pool": nc.gpsimd, "sp": nc.sync}
        P.emit(engobj, sems)
        for osl in range(2):
            name = f"os{osl}"
            nc.sync.wait_ge(dsem[name], P.dma_cnt[name])
    return nc


def _pack_weights(a_w_in, a_w_out, b_w_in, b_w_out):
    groups = []
    A = a_w_in.reshape(KC, 128, 4, JC, 128)
    for j in range(JC):
        groups.append(A[:, :, :, j, :].transpose(1, 2, 0, 3).reshape(128, GW))
    AO = a_w_out.reshape(JC, 128, KC, 128)
    for mm in range(4):
        groups.append(AO[:, :, 2 * mm:2 * mm + 2, :].transpose(1, 2, 0, 3).reshape(128, GW))
    B = b_w_in.reshape(KC, 128, 3, JC, 128)
    for jj in range(8):
        blk = B[:, :, 0:2, 2 * jj:2 * jj + 2, :]
        groups.append(blk.transpose(1, 3, 2, 0, 4).reshape(128, GW))
    for g in range(4):
        blk = B[:, :, 2, 4 * g:4 * g + 4, :]
        groups.append(blk.transpose(1, 2, 0, 3).reshape(128, GW))
    BO = b_w_out.reshape(JC, 128, KC, 128)
    for mm in range(4):
        groups.append(BO[:, :, 2 * mm:2 * mm + 2, :].transpose(1, 2, 0, 3).reshape(128, GW))
    return np.ascontiguousarray(np.stack(groups, axis=0), dtype=np.float32)


def _prep_inputs(x, meta, norm_g, a_w_in, a_conv_w, a_conv_b, a_w_out, b_w_in, b_conv_w, b_conv_b,
                 b_ln_g, b_ln_b, b_w_out, final_g):
    f = lambda a: np.asarray(a, dtype=np.float32)
    x, meta, norm_g, final_g = f(x), f(meta), f(norm_g), f(final_g)
    wpack = _pack_weights(f(a_w_in)[0], f(a_w_out)[0], f(b_w_in)[0], f(b_w_out)[0])
    par = np.concatenate([f(a_conv_w)[0], f(a_conv_b), f(b_conv_w)[0], f(b_conv_b), f(b_ln_g), f(b_ln_b)], axis=0)
    assert par.shape == (NPAR, DI)
    chanp = np.ascontiguousarray(par.reshape(NPAR, JC, 128).transpose(2, 1, 0).reshape(128, JC * NPAR))
    dmp = np.ascontiguousarray(norm_g.reshape(2, KC, 128).transpose(2, 1, 0).reshape(128, KC * 2))
    gfin = np.ascontiguousarray(np.broadcast_to(final_g[None, :], (128, D)))
    in_maps = []
    for c in range(NCORES):
        b = c // 2
        if c % 2 == 0:
            xin = np.concatenate([meta, x[b, :TOK - NMETA]], axis=0)
        else:
            xin = x[b, SEQ - TOK:]
        in_maps.append({"xin": np.ascontiguousarray(xin), "wpack": wpack, "chanp": chanp, "dmp": dmp, "gfin": gfin})
    return in_maps


def kernel(x, meta, norm_g, a_w_in, a_conv_w, a_conv_b, a_w_out, b_w_in, b_conv_w, b_conv_b,
           b_ln_g, b_ln_b, b_w_out, final_g):
    in_maps = _prep_inputs(x, meta, norm_g, a_w_in, a_conv_w, a_conv_b, a_w_out, b_w_in, b_conv_w, b_conv_b,
                           b_ln_g, b_ln_b, b_w_out, final_g)
    nc = build_nc()
    res = run_bass_kernel_spmd(nc, in_maps, core_ids=list(range(NCORES)))
    outs = [np.asarray(r["out"]) for r in res.results]
    full = np.empty((BATCH, SEQ, D), dtype=np.float32)
    split = TOK - NMETA
    for b in range(BATCH):
        full[b, :split] = outs[2 * b][NMETA:]
        full[b, split:] = outs[2 * b + 1][TOK - (SEQ - split):]
    return full
```

```python
import numpy as np
from contextlib import ExitStack

import concourse.bass as bass
import concourse.mybir as mybir
from concourse.bass_utils import run_bass_kernel_spmd

F32 = mybir.dt.float32
BF16 = mybir.dt.bfloat16
I32 = mybir.dt.int32
AF = mybir.ActivationFunctionType
ALU = mybir.AluOpType

D = 1024
DI = 2048
SEQ = 8192
BATCH = 4
NMETA = 16
NCORES = 8
N = 412
NT = 10
TOK = N * NT
HALO = 32
KC = D // 128
JC = DI // 128
NG = 36
GW = 4096
RMS_EPS = 1e-6
LN_EPS = 1e-5
NPAR = 38
KP = 23
NRING = 5
BLK = [(0, 128), (128, 128), (256, 128), (384, N - 384)]

ENGS = ("pe", "act", "dve", "pool", "sp")


class Buf:
    __slots__ = ("name", "lw", "lr")

    def __init__(self, name):
        self.name = name
        self.lw = {}
        self.lr = {}


class Op:
    __slots__ = ("id", "eng", "fn", "waits", "signal", "sigval", "dma_sem", "dma_val", "dma_key")

    def __init__(self, id, eng, fn):
        self.id = id
        self.eng = eng
        self.fn = fn
        self.waits = []
        self.signal = False
        self.sigval = None
        self.dma_sem = None
        self.dma_val = None
        self.dma_key = None


class Prog:
    def __init__(self):
        self.ops = []
        self.waited = {e: {} for e in ENGS}
        self.dma_cnt = {}

    def add(self, eng, fn, reads=(), writes=(), dma=None):
        op = Op(len(self.ops), eng, fn)
        raw = set()
        other = set()
        for b in reads:
            raw.update(b.lw.values())
        for b in writes:
            other.update(b.lw.values())
            other.update(b.lr.values())
        deps = {}
        for d in raw:
            deps[d] = True
        for d in other:
            deps.setdefault(d, False)
        for d in sorted(deps):
            is_raw = deps[d]
            dop = self.ops[d]
            if dop.dma_sem is not None:
                key = ("dma", dop.dma_key)
                if self.waited[eng].get(key, 0) >= dop.dma_val:
                    continue
                self.waited[eng][key] = dop.dma_val
                op.waits.append(("dma", dop.dma_sem, dop.dma_val))
            else:
                p = dop.eng
                if p == eng and dma is None:
                    if eng == "pe":
                        continue
                key = ("eng", p)
                if self.waited[eng].get(key, -1) >= d:
                    continue
                self.waited[eng][key] = d
                dop.signal = True
                op.waits.append(("eng", p, d))
        if dma is not None:
            name, sem = dma
            self.dma_cnt[name] = self.dma_cnt.get(name, 0) + 16
            op.dma_sem = sem
            op.dma_val = self.dma_cnt[name]
            op.dma_key = name
            k = ("dma", name)
        else:
            k = eng
        for b in reads:
            b.lr[k] = op.id
        for b in writes:
            b.lw[k] = op.id
        self.ops.append(op)
        return op

    def emit(self, engobj, sems):
        cnt = {e: 0 for e in ENGS}
        for op in self.ops:
            eng = engobj[op.eng]
            for w in op.waits:
                if w[0] == "eng":
                    eng.wait_ge(sems[w[1]], self.ops[w[2]].sigval)
                else:
                    eng.wait_ge(w[1], w[2])
            ins = op.fn(eng)
            if op.dma_sem is not None:
                ins.then_inc(op.dma_sem, 16)
            elif op.signal:
                cnt[op.eng] += 1
                op.sigval = cnt[op.eng]
                ins.then_inc(sems[op.eng], 1)


class Rot:
    def __init__(self, items):
        self.items = items
        self.i = 0

    def next(self):
        it = self.items[self.i % len(self.items)]
        self.i += 1
        return it


def build_nc(nt=NT, flags=()):
    nc = bass.Bass("TRN2", target_bir_lowering=False)
    xin = nc.dram_tensor("xin", [TOK, D], F32, kind="ExternalInput").ap()
    wpack = nc.dram_tensor("wpack", [NG, 128, GW], F32, kind="ExternalInput").ap()
    chanp_d = nc.dram_tensor("chanp", [128, JC * NPAR], F32, kind="ExternalInput").ap()
    dmp_d = nc.dram_tensor("dmp", [128, KC * 2], F32, kind="ExternalInput").ap()
    gfin_d = nc.dram_tensor("gfin", [128, D], F32, kind="ExternalInput").ap()
    wbf = nc.dram_tensor("wbf", [NG, 128, GW], BF16).ap()
    dgb = nc.dram_tensor("dgb", [JC, 128, KP * 128], BF16).ap()
    out = nc.dram_tensor("out", [TOK, D], F32, kind="ExternalOutput").ap()

    with ExitStack() as es:
        E = es.enter_context
        sb = lambda name, shape, dt: E(nc.sbuf_tensor(name, shape, dt))
        H = sb("H", [128, KC, N], F32)
        HN = sb("HN", [128, KC, N], BF16)
        PC = sb("PC", [128, JC, N + 2], F32)
        G = sb("G", [128, JC, N], BF16)
        U = sb("U", [128, JC, N + 30], BF16)
        DIAG = sb("DIAG", [128, 1, 31, 128], BF16)
        SQ = sb("SQ", [128, KC, N], BF16)
        SZ16 = sb("SZ16", [128, JC, N], BF16)
        WR = sb("WR", [128, NRING, GW], BF16)
        XS = sb("XS", [128, 4, D], F32)
        OS = sb("OS", [128, 2, D], F32)
        JUNK = sb("JUNK", [128, D], BF16)
        TMP = sb("TMP", [128, 14, N], F32)
        CB = sb("CB", [128, 4, N], BF16)
        ST = sb("ST", [128, 5, N], F32)
        CST = sb("CST", [128, 1], F32)
        SM = sb("SM", [128, 8], F32)
        EPS_R = sb("EPS_R", [128, 1], F32)
        identf = sb("identf", [128, 128], F32)
        identb = sb("identb", [128, 128], BF16)
        ones_dm = sb("ones_dm", [128, 128], BF16)
        ones_ch = sb("ones_ch", [128, 128], BF16)
        idx = sb("idx", [128, 128], I32)
        CHP = sb("CHP", [128, JC, NPAR], F32)
        DMP = sb("DMP", [128, KC, 2], F32)
        GFIN = sb("GFIN", [128, D], F32)
        Dp = [E(nc.psum_tensor(f"D{i}", [128, 1024], F32)) for i in range(4)]
        pbig = Dp[3]
        banks = [Dp[i // 2][:, (i % 2) * 512:(i % 2) * 512 + 512] for i in range(6)]

        sems = {e: E(nc.semaphore(f"s_{e}")) for e in ENGS}
        dsem = {}

        def dma_sem(name):
            if name not in dsem:
                dsem[name] = E(nc.semaphore(f"d_{name}"))
            return (name, dsem[name])

        P = Prog()

        Hb = [Buf(f"H{k}") for k in range(KC)]
        HNb = [Buf(f"HN{k}") for k in range(KC)]
        PCbody = [Buf(f"PCb{j}") for j in range(JC)]
        PChalo = [Buf(f"PCh{j}") for j in range(JC)]
        Gb = [Buf(f"G{j}") for j in range(JC)]
        Ubody = [Buf(f"Ub{j}") for j in range(JC)]
        Uhalo = [Buf(f"Uh{j}") for j in range(JC)]
        DIAGb = [Buf("DG0")]
        SQb = [Buf(f"SQ{k}") for k in range(KC)]
        SZb = [Buf(f"SZ{j}") for j in range(JC)]
        WRb = [Buf(f"WR{s}") for s in range(NRING)]
        DGBb = [Buf(f"DGB{j}") for j in range(JC)]
        WBFb = [Buf(f"WBF{g}") for g in range(NG)]
        XSb = Buf("XS")
        OSb = [Buf("OS0"), Buf("OS1")]
        JUNKb = Buf("JUNK")
        S0b = Buf("S0")
        S1b = Buf("S1")
        bankb = [Buf(f"bank{i}") for i in range(6)]
        CONSTb = Buf("const")
        ring = Rot([(banks[i], bankb[i]) for i in range(6)])
        tmpb = [Buf(f"TMP{i}") for i in range(14)]
        ACCr = Rot([(TMP[:, 10 + i, :], tmpb[10 + i]) for i in range(4)])
        XVr = Rot([(TMP[:, i, :], tmpb[i]) for i in (0, 1, 8, 9)])
        SZr = Rot([(TMP[:, 2, :], tmpb[2]), (TMP[:, 3, :], tmpb[3])])
        Ar = Rot([(TMP[:, 4, :], tmpb[4]), (TMP[:, 5, :], tmpb[5])])
        Vr = Rot([(TMP[:, 6, :], tmpb[6]), (TMP[:, 7, :], tmpb[7])])
        SGr = Rot([(TMP[:, 8, :], tmpb[8]), (TMP[:, 9, :], tmpb[9])])
        cbb = [Buf(f"CB{i}") for i in range(4)]
        CBFr = Rot([(CB[:, 0, :], cbb[0]), (CB[:, 1, :], cbb[1])])
        CSQr = Rot([(CB[:, 2, :], cbb[2]), (CB[:, 3, :], cbb[3])])
        stb = [Buf(f"ST{i}") for i in range(5)]
        MS, RSTD, MEAN, M2, MR = [(ST[:, i, :], stb[i]) for i in range(5)]
        smb = [Buf(f"SM{i}") for i in range(8)]
        SSQr = Rot([(SM[:, 0:1], smb[0]), (SM[:, 1:2], smb[1])])
        MS1r = Rot([(SM[:, 2:3], smb[2]), (SM[:, 3:4], smb[3])])
        RS1r = Rot([(SM[:, 4:5], smb[4]), (SM[:, 5:6], smb[5])])
        OSr = Rot([(0, OSb[0]), (1, OSb[1])])
        S0 = pbig[:, 0:N]
        S1 = pbig[:, 512:512 + N]

        P.add("pool", lambda e: e.iota(idx[:], pattern=[[1, 128]], base=0, channel_multiplier=-1), writes=[CONSTb])
        P.add("dve", lambda e: e.tensor_scalar(out=identf[:], in0=idx[:], scalar1=0.0, scalar2=None, op0=ALU.is_equal),
              reads=[CONSTb], writes=[CONSTb])
        P.add("dve", lambda e: e.tensor_copy(out=identb[:], in_=identf[:]), reads=[CONSTb], writes=[CONSTb])
        P.add("dve", lambda e: e.memset(ones_dm[:], 1.0 / D), writes=[CONSTb])
        P.add("dve", lambda e: e.memset(ones_ch[:], 1.0 / DI), writes=[CONSTb])
        P.add("dve", lambda e: e.memset(CST[:], -0.5), writes=[CONSTb])
        P.add("dve", lambda e: e.memset(EPS_R[:], RMS_EPS), writes=[CONSTb])
        P.add("dve", lambda e: e.memset(PC[:, :, 0:2], 0.0), writes=PChalo)
        P.add("dve", lambda e: e.memset(U[:, :, 0:30], 0.0), writes=Uhalo)
        P.add("sp", lambda e: e.dma_start(out=CHP[:], in_=chanp_d.rearrange("p (j c) -> p j c", c=NPAR)),
              writes=[CONSTb], dma=dma_sem("par0"))
        P.add("sp", lambda e: e.dma_start(out=DMP[:], in_=dmp_d.rearrange("p (k c) -> p k c", c=2)),
              writes=[CONSTb], dma=dma_sem("par1"))
        P.add("sp", lambda e: e.dma_start(out=GFIN[:], in_=gfin_d[:, :]), writes=[CONSTb], dma=dma_sem("par2"))

        seq = [("w", g) for g in range(20)]
        for jj in range(JC // 2):
            seq.append(("w", 20 + jj))
            if jj >= 1:
                seq.append(("dg", 2 * jj - 1))
            seq.append(("dg", 2 * jj))
        seq.append(("dg", JC - 1))
        seq += [("w", g) for g in range(28, 36)]
        NSEQ = len(seq)
        state = {"cast": 0, "load": 0, "use": 0}
        loaded_q = {}
        cur_q = {}
        total_groups = nt * NSEQ

        widx = {}
        for kind_, g_ in seq:
            if kind_ == "w":
                widx.setdefault(g_, len(widx))

        def issue_load():
            q = state["load"]
            if q >= total_groups:
                return
            state["load"] += 1
            kind, g = seq[q % NSEQ]
            s = q % NRING
            loaded_q[s] = q
            if kind == "w":
                dfr = widx[g] % 2 == 1
                if q < NSEQ:
                    if dfr and state.get("xs_free", False):
                        P.add("sp", lambda e, g=g: e.dma_start(out=XS[:].rearrange("p b d -> p (b d)"), in_=wpack[g]),
                              writes=[XSb], dma=dma_sem("stg"))

                        def castop(g=g, s=s):
                            P.add("act", lambda e: e.activation(out=WR[:, s, :], in_=XS[:].rearrange("p b d -> p (b d)"),
                                                                func=AF.Copy), reads=[XSb], writes=[WRb[s]])
                        pend_cast.append(castop)
                    else:
                        P.add("pool", lambda e, g=g, s=s: e.dma_start(out=WR[:, s, :], in_=wpack[g]),
                              reads=([XSb] if state.get("after_x", False) else []),
                              writes=[WRb[s]], dma=dma_sem(f"wrs{s}"))
                        if not dfr:
                            pend_store[q] = (g, s)
                elif q < 2 * NSEQ and dfr:
                    P.add("pool", lambda e, g=g, s=s: e.dma_start(out=WR[:, s, :], in_=wpack[g]),
                          writes=[WRb[s]], dma=dma_sem(f"wrs{s}"))
                    pend_store[q] = (g, s)
                else:
                    assert WBFb[g].lw, ("bf16 copy not stored yet", g, q)
                    P.add("sp", lambda e, g=g, s=s: e.dma_start(out=WR[:, s, :], in_=wbf[g]),
                          reads=[WBFb[g]], writes=[WRb[s]], dma=dma_sem(f"wr{s}"))
            else:
                assert DGBb[g].lw, "diag group loaded before it was built"
                P.add("sp", lambda e, g=g, s=s: e.dma_start(out=WR[:, s, 0:KP * 128], in_=dgb[g]),
                      reads=[DGBb[g]], writes=[WRb[s]], dma=dma_sem(f"wr{s}"))

        pend_cast = []
        pend_store = {}

        def flush_stores(upto):
            for q0 in sorted(pend_store):
                if q0 < upto:
                    g0, s0 = pend_store.pop(q0)
                    P.add("sp", lambda e, g0=g0, s0=s0: e.dma_start(out=wbf[g0], in_=WR[:, s0, :]),
                          reads=[WRb[s0]], writes=[WBFb[g0]], dma=dma_sem(f"wst{s0}"))

        def next_group(kind, g, defer=False):
            while pend_cast:
                pend_cast.pop(0)()
            q = state["use"]
            flush_stores(q)
            assert seq[q % NSEQ] == (kind, g), (q, seq[q % NSEQ], kind, g)
            state["use"] += 1
            if not defer:
                issue_load()
            cur_q[q % NRING] = q
            return q % NRING

        def load_x(t):
            if t >= nt:
                return
            tk = t * N
            P.add("sp", lambda e: e.dma_start(out=XS[:, 0:3, :],
                                              in_=xin[tk:tk + 384, :].rearrange("(b p) d -> p b d", p=128)),
                  writes=[XSb], dma=dma_sem("xsa"))
            P.add("sp", lambda e: e.dma_start(out=XS[0:N - 384, 3, :], in_=xin[tk + 384:tk + N, :]),
                  writes=[XSb], dma=dma_sem("xsb"))

        load_x(0)
        state["after_x"] = True
        for _ in range(NRING - 1):
            issue_load()
        state["after_x"] = False

        def build_diag(j):
            ds = 0

            def f(e):
                for k in range(KP):
                    ins = e.tensor_scalar(out=DIAG[:, ds, k, :], in0=identb[:], scalar1=CHP[:, j, 4 + k:5 + k],
                                          scalar2=None, op0=ALU.mult)
                return ins
            P.add("dve", f, reads=[CONSTb], writes=[DIAGb[ds]])
            P.add("sp", lambda e: e.dma_start(out=dgb[j], in_=DIAG[:, ds, 0:KP, :].rearrange("p k m -> p (k m)")),
                  reads=[DIAGb[ds]], writes=[DGBb[j]], dma=dma_sem(f"dgst{ds}"))

        def wsl(s, i):
            return WR[:, s, i * 128:(i + 1) * 128]

        def stats_act(kc, src_ap, src_bufs):
            P.add("act", lambda e: e.activation(out=SQ[:, kc, :], in_=src_ap, func=AF.Square),
                  reads=src_bufs, writes=[SQb[kc]])

        def stats_mm(kc):
            P.add("pe", lambda e: e.matmul(S0, lhsT=ones_dm[:], rhs=SQ[:, kc, :], start=(kc == 0), stop=(kc == KC - 1)),
                  reads=[SQb[kc], CONSTb], writes=[S0b])

        def preload_ln():
            P.add("act", lambda e: e.activation(out=SM[:, 6:7], in_=EPS_R[:, 0:1], func=AF.Ln),
                  reads=[CONSTb], writes=[smb[6]])

        def rms_finish(layer):
            P.add("act", lambda e: e.activation(out=MS[0], in_=S0, func=AF.Ln, bias=EPS_R[:, 0:1], scale=1.0),
                  reads=[S0b, CONSTb], writes=[MS[1]])
            P.add("act", lambda e: e.activation(out=RSTD[0], in_=MS[0], func=AF.Exp, scale=-0.5),
                  reads=[MS[1]], writes=[RSTD[1]])
            for kc in range(KC):
                P.add("dve", lambda e, kc=kc: e.scalar_tensor_tensor(
                    out=HN[:, kc, :], in0=H[:, kc, :], scalar=DMP[:, kc, layer:layer + 1], in1=RSTD[0],
                    op0=ALU.mult, op1=ALU.mult), reads=[Hb[kc], RSTD[1], CONSTb], writes=[HNb[kc]])

        def chk(slot):
            assert loaded_q[slot] == cur_q[slot], (slot, loaded_q[slot], cur_q[slot])

        def proj(slot, wi0, rhs_t, nk, reads):
            chk(slot)
            bank, bb = ring.next()

            def f(e):
                for kc in range(nk):
                    ins = e.matmul(bank[:, 0:N], lhsT=wsl(slot, wi0 + kc), rhs=rhs_t[:, kc, :],
                                   start=(kc == 0), stop=(kc == nk - 1))
                return ins
            P.add("pe", f, reads=[WRb[slot]] + reads, writes=[bb])
            return bank[:, 0:N], bb

        def proj_multi(slot, wi0s, rhs_t, nk, rbufs):
            chk(slot)
            outs = [ring.next() for _ in wi0s]
            for kc in range(nk):
                def f(e, kc=kc):
                    for (bank, bb), wi0 in zip(outs, wi0s):
                        ins = e.matmul(bank[:, 0:N], lhsT=wsl(slot, wi0 + kc), rhs=rhs_t[:, kc, :],
                                       start=(kc == 0), stop=(kc == nk - 1))
                    return ins
                P.add("pe", f, reads=[WRb[slot], rbufs[kc]], writes=[bb for _, bb in outs])
            return [(bank[:, 0:N], bb) for bank, bb in outs]

        def out_proj_m(wg0):
            pend = None
            for m in range(KC):
                if m % 2 == 0:
                    slot = next_group("w", wg0 + m // 2)
                ml = m % 2
                ps, pb = proj(slot, ml * JC, G, JC, Gb)
                if pend is not None:
                    stats_mm(pend)
                P.add("dve", lambda e, m=m, ps=ps: e.tensor_tensor(out=H[:, m, :], in0=H[:, m, :], in1=ps, op=ALU.add),
                      reads=[Hb[m], pb], writes=[Hb[m]])
                stats_act(m, H[:, m, :], [Hb[m]])
                pend = m
            stats_mm(pend)

        def tile(t):
            tok0 = t * N
            pend = None
            for kc in range(KC):
                bank, bb = ring.next()

                def f(e, kc=kc, bank=bank):
                    for b, (o, nb) in enumerate(BLK):
                        ins = e.transpose(out=bank[:, o:o + nb], in_=XS[0:nb, b, kc * 128:(kc + 1) * 128],
                                          identity=identf[0:nb, 0:nb])
                    return ins
                P.add("pe", f, reads=[XSb, CONSTb], writes=[bb])
                if pend is not None:
                    stats_mm(pend)
                P.add("act", lambda e, kc=kc, bank=bank: e.activation(out=H[:, kc, :], in_=bank[:, 0:N], func=AF.Copy),
                      reads=[bb], writes=[Hb[kc]])
                stats_act(kc, bank[:, 0:N], [bb])
                pend = kc
            stats_mm(pend)
            if t > 0:
                load_x(t + 1)
            else:
                state["xs_free"] = True

            rms_finish(0)
            for j in range(JC):
                slot = next_group("w", j)
                if t == 0:
                    build_diag(j)
                if j == 0:
                    (xv_ps, xv_b), (cg_ps, cg_b), (z_ps, z_b), (bg_ps, bg_b) = proj_multi(
                        slot, [2 * KC, 1 * KC, 3 * KC, 0 * KC], HN, KC, HNb)
                else:
                    xv_ps, xv_b = proj(slot, 2 * KC, HN, KC, HNb)
                    cg_ps, cg_b = proj(slot, 1 * KC, HN, KC, HNb)
                    z_ps, z_b = proj(slot, 3 * KC, HN, KC, HNb)
                    bg_ps, bg_b = proj(slot, 0 * KC, HN, KC, HNb)
                xv, xvb = XVr.next()
                sz, szb = SZr.next()
                a, ab = Ar.next()
                v, vb = Vr.next()
                P.add("act", lambda e, xv=xv, xv_ps=xv_ps: e.activation(out=xv, in_=xv_ps, func=AF.Copy),
                      reads=[xv_b], writes=[xvb])
                P.add("act", lambda e, sz=sz, z_ps=z_ps: e.activation(out=sz, in_=z_ps, func=AF.Silu),
                      reads=[z_b], writes=[szb])
                P.add("dve", lambda e, j=j, cg_ps=cg_ps, xv=xv: e.tensor_tensor(
                    out=PC[:, j, 2:N + 2], in0=cg_ps, in1=xv, op=ALU.mult),
                    reads=[cg_b, xvb], writes=[PCbody[j]])
                P.add("dve", lambda e, j=j, a=a: e.tensor_scalar(
                    out=a, in0=PC[:, j, 0:N], scalar1=CHP[:, j, 0:1], scalar2=CHP[:, j, 3:4],
                    op0=ALU.mult, op1=ALU.add), reads=[PCbody[j], PChalo[j], CONSTb], writes=[ab])
                for k in (1, 2):
                    P.add("dve", lambda e, j=j, a=a, k=k: e.scalar_tensor_tensor(
                        out=a, in0=PC[:, j, k:N + k], scalar=CHP[:, j, k:k + 1], in1=a,
                        op0=ALU.mult, op1=ALU.add), reads=[PCbody[j], PChalo[j], ab, CONSTb], writes=[ab])
                P.add("dve", lambda e, v=v, bg_ps=bg_ps, a=a: e.tensor_tensor(out=v, in0=bg_ps, in1=a, op=ALU.mult),
                      reads=[bg_b, ab], writes=[vb])
                P.add("dve", lambda e, j=j, v=v, sz=sz: e.tensor_tensor(out=G[:, j, :], in0=v, in1=sz, op=ALU.mult),
                      reads=[vb, szb], writes=[Gb[j]])
                P.add("pool", lambda e, j=j: e.tensor_copy(out=PC[:, j, 0:2], in_=PC[:, j, N:N + 2]),
                      reads=[PCbody[j]], writes=[PChalo[j]])
            preload_ln()
            out_proj_m(16)

            rms_finish(1)
            pend_stats = []

            def conv_pe(j):
                dslot = next_group("dg", j)
                chk(dslot)
                bank, bb = ring.next()

                def f(e):
                    for k in range(KP):
                        ins = e.matmul(bank[:, 0:N], lhsT=wsl(dslot, k), rhs=U[:, j, k:k + N],
                                       start=(k == 0), stop=(k == KP - 1))
                    return ins
                P.add("pe", f, reads=[Ubody[j], Uhalo[j], WRb[dslot]], writes=[bb])
                return [j, bank[:, 0:N], bb]

            def conv_tap(c, k):
                j, src, srcb, acc, accb = c
                P.add("dve", lambda e: e.scalar_tensor_tensor(
                    out=acc, in0=U[:, j, k:k + N], scalar=CHP[:, j, 4 + k:5 + k], in1=src,
                    op0=ALU.mult, op1=ALU.add), reads=[Ubody[j], Uhalo[j], srcb, CONSTb], writes=[accb])
                c[1], c[2] = acc, accb

            def conv_first(ctxs):
                if KP < 31:
                    for c in ctxs:
                        c.extend(ACCr.next())
                        conv_tap(c, KP)

            def conv_post(ctxs):
                while pend_stats:
                    pend_stats.pop(0)()
                for k in range(KP + 1, 31):
                    for c in ctxs:
                        conv_tap(c, k)
                ctxs = [c[0:3] for c in ctxs]
                for j, src, srcb in ctxs:
                    conv_evac(j, src, srcb)

            def conv_evac(j, src, srcb):
                cbf, cbfb = CBFr.next()
                csq, csqb = CSQr.next()
                P.add("act", lambda e: e.activation(out=PC[:, j, 2:N + 2], in_=src, func=AF.Identity,
                                                    bias=CHP[:, j, 35:36], scale=0.5),
                      reads=[srcb, CONSTb], writes=[PCbody[j]])
                P.add("act", lambda e: e.activation(out=csq, in_=src, func=AF.Square,
                                                    bias=CHP[:, j, 35:36], scale=0.5),
                      reads=[srcb, CONSTb], writes=[csqb])
                P.add("act", lambda e: e.activation(out=cbf, in_=src, func=AF.Identity,
                                                    bias=CHP[:, j, 35:36], scale=0.5),
                      reads=[srcb, CONSTb], writes=[cbfb])

                def g(e):
                    e.matmul(S0, lhsT=ones_ch[:], rhs=cbf, start=(j == 0), stop=(j == JC - 1))
                    return e.matmul(S1, lhsT=ones_ch[:], rhs=csq, start=(j == 0), stop=(j == JC - 1))
                pend_stats.append(lambda: P.add("pe", g, reads=[cbfb, csqb, CONSTb], writes=[S0b, S1b]))
                P.add("pool", lambda e: e.tensor_copy(out=U[:, j, 0:30], in_=U[:, j, N:N + 30]),
                      reads=[Ubody[j]], writes=[Uhalo[j]])

            def phaseA(j, slot, pre=None):
                jl = j % 2
                if pre is not None:
                    (ua_ps, ua_b), (ub_ps, ub_b) = pre
                else:
                    ua_ps, ua_b = proj(slot, (jl * 2 + 0) * KC, HN, KC, HNb)
                    ub_ps, ub_b = proj(slot, (jl * 2 + 1) * KC, HN, KC, HNb)
                sg, sgb = SGr.next()
                P.add("act", lambda e: e.activation(out=sg, in_=ub_ps, func=AF.Tanh, scale=0.5),
                      reads=[ub_b], writes=[sgb])
                P.add("dve", lambda e: e.scalar_tensor_tensor(out=U[:, j, 30:N + 30], in0=sg, scalar=1.0, in1=ua_ps,
                                                              op0=ALU.add, op1=ALU.mult),
                      reads=[ua_b, sgb], writes=[Ubody[j]])

            def zproj_pe(j, slot):
                return (j,) + proj(slot, (j % 4) * KC, HN, KC, HNb)

            def zproj_act(j, z_ps, z_b):
                P.add("act", lambda e: e.activation(out=SZ16[:, j, :], in_=z_ps, func=AF.Silu),
                      reads=[z_b], writes=[SZb[j]])

            def zproj(j, slot):
                zproj_act(*zproj_pe(j, slot))

            pend_post = None
            for jj in range(JC // 2):
                slot = next_group("w", 20 + jj)
                if jj == 0:
                    r4 = proj_multi(slot, [0, KC, 2 * KC, 3 * KC], HN, KC, HNb)
                    phaseA(0, slot, pre=r4[0:2])
                    phaseA(1, slot, pre=r4[2:4])
                else:
                    phaseA(2 * jj, slot)
                    phaseA(2 * jj + 1, slot)
                if pend_post is not None:
                    conv_post(pend_post)
                cs = []
                if jj >= 1:
                    cs.append(conv_pe(2 * jj - 1))
                cs.append(conv_pe(2 * jj))
                conv_first(cs)
                pend_post = cs
            conv_post(pend_post)
            cs = [conv_pe(JC - 1)]
            conv_first(cs)
            preload_ln()
            ZLEAD = 5
            zlead = []
            for j in range(ZLEAD):
                if j % 4 == 0:
                    zslot = next_group("w", 28 + j // 4)
                zlead.append(zproj_pe(j, zslot))
                if j == 2:
                    conv_post(cs)
            while pend_stats:
                pend_stats.pop(0)()
            P.add("dve", lambda e: e.tensor_copy(out=MEAN[0], in_=S0), reads=[S0b], writes=[MEAN[1]])
            P.add("dve", lambda e: e.tensor_tensor(out=M2[0], in0=MEAN[0], in1=MEAN[0], op=ALU.mult),
                  reads=[MEAN[1]], writes=[M2[1]])
            P.add("dve", lambda e: e.scalar_tensor_tensor(out=MS[0], in0=S1, scalar=LN_EPS, in1=M2[0],
                                                          op0=ALU.add, op1=ALU.subtract),
                  reads=[S1b, M2[1]], writes=[MS[1]])
            P.add("act", lambda e: e.activation(out=M2[0], in_=MS[0], func=AF.Ln), reads=[MS[1]], writes=[M2[1]])
            P.add("act", lambda e: e.activation(out=RSTD[0], in_=M2[0], func=AF.Exp, scale=-0.5),
                  reads=[M2[1]], writes=[RSTD[1]])
            for zl in zlead:
                zproj_act(*zl)
            P.add("dve", lambda e: e.tensor_tensor(out=MR[0], in0=MEAN[0], in1=RSTD[0], op=ALU.mult),
                  reads=[MEAN[1], RSTD[1]], writes=[MR[1]])

            def norm_a(j):
                a, ab = Ar.next()
                v, vb = Vr.next()
                xv, xvb = XVr.next()
                P.add("dve", lambda e: e.tensor_tensor(out=a, in0=PC[:, j, 2:N + 2], in1=RSTD[0], op=ALU.mult),
                      reads=[PCbody[j], RSTD[1]], writes=[ab])
                P.add("dve", lambda e: e.tensor_tensor(out=v, in0=a, in1=MR[0], op=ALU.subtract),
                      reads=[ab, MR[1]], writes=[vb])
                P.add("act", lambda e: e.activation(out=xv, in_=v, func=AF.Silu,
                                                    bias=CHP[:, j, 37:38], scale=CHP[:, j, 36:37]),
                      reads=[vb, CONSTb], writes=[xvb])
                return xv, xvb

            def norm_g(j, xv, xvb):
                P.add("dve", lambda e: e.tensor_tensor(out=G[:, j, :], in0=xv, in1=SZ16[:, j, :], op=ALU.mult),
                      reads=[xvb, SZb[j]], writes=[Gb[j]])

            SKEW = 2
            pendn = []
            for j in range(JC):
                jz = j + ZLEAD
                if jz < JC:
                    if jz % 4 == 0:
                        zslot = next_group("w", 28 + jz // 4)
                    zproj(jz, zslot)
                pendn.append((j,) + norm_a(j))
                if len(pendn) > SKEW:
                    norm_g(*pendn.pop(0))
            while pendn:
                norm_g(*pendn.pop(0))

            oslots = [next_group("w", 32, defer=False), next_group("w", 33, defer=True)]
            for sl in oslots:
                chk(sl)
            accs = [ring.next() for _ in range(4)]
            for j in range(JC):
                def f(e, j=j):
                    for m in range(4):
                        ins = e.matmul(accs[m][0][:, 0:N], lhsT=wsl(oslots[m // 2], (m % 2) * JC + j), rhs=G[:, j, :],
                                       start=(j == 0), stop=(j == JC - 1))
                    return ins
                P.add("pe", f, reads=[Gb[j]] + [WRb[sl] for sl in oslots], writes=[ab_ for _, ab_ in accs])
            issue_load()
            for m in range(4):
                P.add("dve", lambda e, m=m: e.tensor_tensor(out=H[:, m, :], in0=H[:, m, :], in1=accs[m][0][:, 0:N], op=ALU.add),
                      reads=[Hb[m], accs[m][1]], writes=[Hb[m]])
            for m in range(4, KC):
                if m % 2 == 0:
                    slot = next_group("w", 32 + m // 2)
                ps, pb = proj(slot, (m % 2) * JC, G, JC, Gb)
                P.add("dve", lambda e, m=m, ps=ps: e.tensor_tensor(out=H[:, m, :], in0=H[:, m, :], in1=ps, op=ALU.add),
                      reads=[Hb[m], pb], writes=[Hb[m]])

            preload_ln()
            dbl = [(Dp[3], [S0b, S1b]), (Dp[2], [bankb[4], bankb[5]])]
            for b, (o, nb) in enumerate(BLK):
                pt, ptb = dbl[b % 2]

                def f(e, o=o, nb=nb, pt=pt):
                    for kc in range(KC):
                        ins = e.transpose(out=pt[0:nb, kc * 128:(kc + 1) * 128], in_=H[:, kc, o:o + nb],
                                          identity=identf[:, :])
                    return ins
                P.add("pe", f, reads=Hb + [CONSTb], writes=ptb)
                ssq, ssqb = SSQr.next()
                ms1, ms1b = MS1r.next()
                rs1, rs1b = RS1r.next()
                osl, osb = OSr.next()
                P.add("act", lambda e, nb=nb, ssq=ssq, pt=pt: e.activation(out=JUNK[0:nb, :], in_=pt[0:nb, :], func=AF.Square,
                                                                         accum_out=ssq[0:nb, :]),
                      reads=ptb, writes=[JUNKb, ssqb])
                P.add("dve", lambda e, nb=nb, ssq=ssq, ms1=ms1: e.tensor_scalar(
                    out=ms1[0:nb, :], in0=ssq[0:nb, :], scalar1=1.0 / D, scalar2=RMS_EPS, op0=ALU.mult, op1=ALU.add),
                    reads=[ssqb], writes=[ms1b])
                P.add("pool", lambda e, nb=nb, ms1=ms1, rs1=rs1: e.tensor_tensor(
                    out=rs1[0:nb, :], in0=ms1[0:nb, :], in1=CST[0:nb, 0:1], op=ALU.pow),
                    reads=[ms1b, CONSTb], writes=[rs1b])
                P.add("dve", lambda e, nb=nb, rs1=rs1, osl=osl, pt=pt: e.scalar_tensor_tensor(
                    out=OS[0:nb, osl, :], in0=pt[0:nb, :], scalar=rs1[0:nb, :], in1=GFIN[0:nb, :],
                    op0=ALU.mult, op1=ALU.mult), reads=ptb + [rs1b, CONSTb], writes=[osb])
                P.add("sp", lambda e, nb=nb, o=o, osl=osl: e.dma_start(out=out[tok0 + o:tok0 + o + nb, :],
                                                                     in_=OS[0:nb, osl, :]),
                      reads=[osb], dma=dma_sem(f"os{osl}"))

        for t in range(nt):
            tile(t)
            if t == 0:
                while pend_cast:
                    pend_cast.pop(0)()
                load_x(1)

        engobj = {"pe": nc.tensor, "act": nc.scalar, "dve": nc.vector, "pool": nc.gpsimd, "sp": nc.sync}
        P.emit(engobj, sems)
        for osl in range(2):
            name = f"os{osl}"
            nc.sync.wait_ge(dsem[name], P.dma_cnt[name])
    return nc


def _pack_weights(a_w_in, a_w_out, b_w_in, b_w_out):
    groups = []
    A = a_w_in.reshape(KC, 128, 4, JC, 128)
    for j in range(JC):
        groups.append(A[:, :, :, j, :].transpose(1, 2, 0, 3).reshape(128, GW))
    AO = a_w_out.reshape(JC, 128, KC, 128)
    for mm in range(4):
        groups.append(AO[:, :, 2 * mm:2 * mm + 2, :].transpose(1, 2, 0, 3).reshape(128, GW))
    B = b_w_in.reshape(KC, 128, 3, JC, 128)
    for jj in range(8):
        blk = B[:, :, 0:2, 2 * jj:2 * jj + 2, :]
        groups.append(blk.transpose(1, 3, 2, 0, 4).reshape(128, GW))
    for g in range(4):
        blk = B[:, :, 2, 4 * g:4 * g + 4, :]
        groups.append(blk.transpose(1, 2, 0, 3).reshape(128, GW))
    BO = b_w_out.reshape(JC, 128, KC, 128)
    for mm in range(4):
        groups.append(BO[:, :, 2 * mm:2 * mm + 2, :].transpose(1, 2, 0, 3).reshape(128, GW))
    return np.ascontiguousarray(np.stack(groups, axis=0), dtype=np.float32)


def _prep_inputs(x, meta, norm_g, a_w_in, a_conv_w, a_conv_b, a_w_out, b_w_in, b_conv_w, b_conv_b,
                 b_ln_g, b_ln_b, b_w_out, final_g):
    f = lambda a: np.asarray(a, dtype=np.float32)
    x, meta, norm_g, final_g = f(x), f(meta), f(norm_g), f(final_g)
    wpack = _pack_weights(f(a_w_in)[0], f(a_w_out)[0], f(b_w_in)[0], f(b_w_out)[0])
    par = np.concatenate([f(a_conv_w)[0], f(a_conv_b), f(b_conv_w)[0], f(b_conv_b), f(b_ln_g), f(b_ln_b)], axis=0)
    assert par.shape == (NPAR, DI)
    chanp = np.ascontiguousarray(par.reshape(NPAR, JC, 128).transpose(2, 1, 0).reshape(128, JC * NPAR))
    dmp = np.ascontiguousarray(norm_g.reshape(2, KC, 128).transpose(2, 1, 0).reshape(128, KC * 2))
    gfin = np.ascontiguousarray(np.broadcast_to(final_g[None, :], (128, D)))
    in_maps = []
    for c in range(NCORES):
        b = c // 2
        if c % 2 == 0:
            xin = np.concatenate([meta, x[b, :TOK - NMETA]], axis=0)
        else:
            xin = x[b, SEQ - TOK:]
        in_maps.append({"xin": np.ascontiguousarray(xin), "wpack": wpack, "chanp": chanp, "dmp": dmp, "gfin": gfin})
    return in_maps


def kernel(x, meta, norm_g, a_w_in, a_conv_w, a_conv_b, a_w_out, b_w_in, b_conv_w, b_conv_b,
           b_ln_g, b_ln_b, b_w_out, final_g):
    in_maps = _prep_inputs(x, meta, norm_g, a_w_in, a_conv_w, a_conv_b, a_w_out, b_w_in, b_conv_w, b_conv_b,
                           b_ln_g, b_ln_b, b_w_out, final_g)
    nc = build_nc()
    res = run_bass_kernel_spmd(nc, in_maps, core_ids=list(range(NCORES)))
    outs = [np.asarray(r["out"]) for r in res.results]
    full = np.empty((BATCH, SEQ, D), dtype=np.float32)
    split = TOK - NMETA
    for b in range(BATCH):
        full[b, :split] = outs[2 * b][NMETA:]
        full[b, split:] = outs[2 * b + 1][TOK - (SEQ - split):]
    return full
```

```python
import numpy as np
from contextlib import ExitStack

import concourse.bass as bass
import concourse.mybir as mybir
from concourse.bass_utils import run_bass_kernel_spmd

F32 = mybir.dt.float32
BF16 = mybir.dt.bfloat16
I32 = mybir.dt.int32
AF = mybir.ActivationFunctionType
ALU = mybir.AluOpType

D = 1024
DI = 2048
SEQ = 8192
BATCH = 4
NMETA = 16
NCORES = 8
N = 412
NT = 10
TOK = N * NT
HALO = 32
KC = D // 128
JC = DI // 128
NG = 36
GW = 4096
RMS_EPS = 1e-6
LN_EPS = 1e-5
NPAR = 38
KP = 23
NRING = 5
BLK = [(0, 128), (128, 128), (256, 128), (384, N - 384)]

ENGS = ("pe", "act", "dve", "pool", "sp")


class Buf:
    __slots__ = ("name", "lw", "lr")

    def __init__(self, name):
        self.name = name
        self.lw = {}
        self.lr = {}


class Op:
    __slots__ = ("id", "eng", "fn", "waits", "signal", "sigval", "dma_sem", "dma_val", "dma_key")

    def __init__(self, id, eng, fn):
        self.id = id
        self.eng = eng
        self.fn = fn
        self.waits = []
        self.signal = False
        self.sigval = None
        self.dma_sem = None
        self.dma_val = None
        self.dma_key = None


class Prog:
    def __init__(self):
        self.ops = []
        self.waited = {e: {} for e in ENGS}
        self.dma_cnt = {}

    def add(self, eng, fn, reads=(), writes=(), dma=None):
        op = Op(len(self.ops), eng, fn)
        raw = set()
        other = set()
        for b in reads:
            raw.update(b.lw.values())
        for b in writes:
            other.update(b.lw.values())
            other.update(b.lr.values())
        deps = {}
        for d in raw:
            deps[d] = True
        for d in other:
            deps.setdefault(d, False)
        for d in sorted(deps):
            is_raw = deps[d]
            dop = self.ops[d]
            if dop.dma_sem is not None:
                key = ("dma", dop.dma_key)
                if self.waited[eng].get(key, 0) >= dop.dma_val:
                    continue
                self.waited[eng][key] = dop.dma_val
                op.waits.append(("dma", dop.dma_sem, dop.dma_val))
            else:
                p = dop.eng
                if p == eng and dma is None:
                    if eng == "pe":
                        continue
                key = ("eng", p)
                if self.waited[eng].get(key, -1) >= d:
                    continue
                self.waited[eng][key] = d
                dop.signal = True
                op.waits.append(("eng", p, d))
        if dma is not None:
            name, sem = dma
            self.dma_cnt[name] = self.dma_cnt.get(name, 0) + 16
            op.dma_sem = sem
            op.dma_val = self.dma_cnt[name]
            op.dma_key = name
            k = ("dma", name)
        else:
            k = eng
        for b in reads:
            b.lr[k] = op.id
        for b in writes:
            b.lw[k] = op.id
        self.ops.append(op)
        return op

    def emit(self, engobj, sems):
        cnt = {e: 0 for e in ENGS}
        for op in self.ops:
            eng = engobj[op.eng]
            for w in op.waits:
                if w[0] == "eng":
                    eng.wait_ge(sems[w[1]], self.ops[w[2]].sigval)
                else:
                    eng.wait_ge(w[1], w[2])
            ins = op.fn(eng)
            if op.dma_sem is not None:
                ins.then_inc(op.dma_sem, 16)
            elif op.signal:
                cnt[op.eng] += 1
                op.sigval = cnt[op.eng]
                ins.then_inc(sems[op.eng], 1)


class Rot:
    def __init__(self, items):
        self.items = items
        self.i = 0

    def next(self):
        it = self.items[self.i % len(self.items)]
        self.i += 1
        return it


def build_nc(nt=NT, flags=()):
    nc = bass.Bass("TRN2", target_bir_lowering=False)
    xin = nc.dram_tensor("xin", [TOK, D], F32, kind="ExternalInput").ap()
    wpack = nc.dram_tensor("wpack", [NG, 128, GW], F32, kind="ExternalInput").ap()
    chanp_d = nc.dram_tensor("chanp", [128, JC * NPAR], F32, kind="ExternalInput").ap()
    dmp_d = nc.dram_tensor("dmp", [128, KC * 2], F32, kind="ExternalInput").ap()
    gfin_d = nc.dram_tensor("gfin", [128, D], F32, kind="ExternalInput").ap()
    wbf = nc.dram_tensor("wbf", [NG, 128, GW], BF16).ap()
    dgb = nc.dram_tensor("dgb", [JC, 128, KP * 128], BF16).ap()
    out = nc.dram_tensor("out", [TOK, D], F32, kind="ExternalOutput").ap()

    with ExitStack() as es:
        E = es.enter_context
        sb = lambda name, shape, dt: E(nc.sbuf_tensor(name, shape, dt))
        H = sb("H", [128, KC, N], F32)
        HN = sb("HN", [128, KC, N], BF16)
        PC = sb("PC", [128, JC, N + 2], F32)
        G = sb("G", [128, JC, N], BF16)
        U = sb("U", [128, JC, N + 30], BF16)
        DIAG = sb("DIAG", [128, 1, 31, 128], BF16)
        SQ = sb("SQ", [128, KC, N], BF16)
        SZ16 = sb("SZ16", [128, JC, N], BF16)
        WR = sb("WR", [128, NRING, GW], BF16)
        XS = sb("XS", [128, 4, D], F32)
        OS = sb("OS", [128, 2, D], F32)
        JUNK = sb("JUNK", [128, D], BF16)
        TMP = sb("TMP", [128, 14, N], F32)
        CB = sb("CB", [128, 4, N], BF16)
        ST = sb("ST", [128, 5, N], F32)
        CST = sb("CST", [128, 1], F32)
        SM = sb("SM", [128, 8], F32)
        EPS_R = sb("EPS_R", [128, 1], F32)
        identf = sb("identf", [128, 128], F32)
        identb = sb("identb", [128, 128], BF16)
        ones_dm = sb("ones_dm", [128, 128], BF16)
        ones_ch = sb("ones_ch", [128, 128], BF16)
        idx = sb("idx", [128, 128], I32)
        CHP = sb("CHP", [128, JC, NPAR], F32)
        DMP = sb("DMP", [128, KC, 2], F32)
        GFIN = sb("GFIN", [128, D], F32)
        Dp = [E(nc.psum_tensor(f"D{i}", [128, 1024], F32)) for i in range(4)]
        pbig = Dp[3]
        banks = [Dp[i // 2][:, (i % 2) * 512:(i % 2) * 512 + 512] for i in range(6)]

        sems = {e: E(nc.semaphore(f"s_{e}")) for e in ENGS}
        dsem = {}

        def dma_sem(name):
            if name not in dsem:
                dsem[name] = E(nc.semaphore(f"d_{name}"))
            return (name, dsem[name])

        P = Prog()

        Hb = [Buf(f"H{k}") for k in range(KC)]
        HNb = [Buf(f"HN{k}") for k in range(KC)]
        PCbody = [Buf(f"PCb{j}") for j in range(JC)]
        PChalo = [Buf(f"PCh{j}") for j in range(JC)]
        Gb = [Buf(f"G{j}") for j in range(JC)]
        Ubody = [Buf(f"Ub{j}") for j in range(JC)]
        Uhalo = [Buf(f"Uh{j}") for j in range(JC)]
        DIAGb = [Buf("DG0")]
        SQb = [Buf(f"SQ{k}") for k in range(KC)]
        SZb = [Buf(f"SZ{j}") for j in range(JC)]
        WRb = [Buf(f"WR{s}") for s in range(NRING)]
        DGBb = [Buf(f"DGB{j}") for j in range(JC)]
        WBFb = [Buf(f"WBF{g}") for g in range(NG)]
        XSb = Buf("XS")
        OSb = [Buf("OS0"), Buf("OS1")]
        JUNKb = Buf("JUNK")
        S0b = Buf("S0")
        S1b = Buf("S1")
        bankb = [Buf(f"bank{i}") for i in range(6)]
        CONSTb = Buf("const")
        ring = Rot([(banks[i], bankb[i]) for i in range(6)])
        tmpb = [Buf(f"TMP{i}") for i in range(14)]
        ACCr = Rot([(TMP[:, 10 + i, :], tmpb[10 + i]) for i in range(4)])
        XVr = Rot([(TMP[:, i, :], tmpb[i]) for i in (0, 1, 8, 9)])
        SZr = Rot([(TMP[:, 2, :], tmpb[2]), (TMP[:, 3, :], tmpb[3])])
        Ar = Rot([(TMP[:, 4, :], tmpb[4]), (TMP[:, 5, :], tmpb[5])])
        Vr = Rot([(TMP[:, 6, :], tmpb[6]), (TMP[:, 7, :], tmpb[7])])
        SGr = Rot([(TMP[:, 8, :], tmpb[8]), (TMP[:, 9, :], tmpb[9])])
        cbb = [Buf(f"CB{i}") for i in range(4)]
        CBFr = Rot([(CB[:, 0, :], cbb[0]), (CB[:, 1, :], cbb[1])])
        CSQr = Rot([(CB[:, 2, :], cbb[2]), (CB[:, 3, :], cbb[3])])
        stb = [Buf(f"ST{i}") for i in range(5)]
        MS, RSTD, MEAN, M2, MR = [(ST[:, i, :], stb[i]) for i in range(5)]
        smb = [Buf(f"SM{i}") for i in range(8)]
        SSQr = Rot([(SM[:, 0:1], smb[0]), (SM[:, 1:2], smb[1])])
        MS1r = Rot([(SM[:, 2:3], smb[2]), (SM[:, 3:4], smb[3])])
        RS1r = Rot([(SM[:, 4:5], smb[4]), (SM[:, 5:6], smb[5])])
        OSr = Rot([(0, OSb[0]), (1, OSb[1])])
        S0 = pbig[:, 0:N]
        S1 = pbig[:, 512:512 + N]

        P.add("pool", lambda e: e.iota(idx[:], pattern=[[1, 128]], base=0, channel_multiplier=-1), writes=[CONSTb])
        P.add("dve", lambda e: e.tensor_scalar(out=identf[:], in0=idx[:], scalar1=0.0, scalar2=None, op0=ALU.is_equal),
              reads=[CONSTb], writes=[CONSTb])
        P.add("dve", lambda e: e.tensor_copy(out=identb[:], in_=identf[:]), reads=[CONSTb], writes=[CONSTb])
        P.add("dve", lambda e: e.memset(ones_dm[:], 1.0 / D), writes=[CONSTb])
        P.add("dve", lambda e: e.memset(ones_ch[:], 1.0 / DI), writes=[CONSTb])
        P.add("dve", lambda e: e.memset(CST[:], -0.5), writes=[CONSTb])
        P.add("dve", lambda e: e.memset(EPS_R[:], RMS_EPS), writes=[CONSTb])
        P.add("dve", lambda e: e.memset(PC[:, :, 0:2], 0.0), writes=PChalo)
        P.add("dve", lambda e: e.memset(U[:, :, 0:30], 0.0), writes=Uhalo)
        P.add("sp", lambda e: e.dma_start(out=CHP[:], in_=chanp_d.rearrange("p (j c) -> p j c", c=NPAR)),
              writes=[CONSTb], dma=dma_sem("par0"))
        P.add("sp", lambda e: e.dma_start(out=DMP[:], in_=dmp_d.rearrange("p (k c) -> p k c", c=2)),
              writes=[CONSTb], dma=dma_sem("par1"))
        P.add("sp", lambda e: e.dma_start(out=GFIN[:], in_=gfin_d[:, :]), writes=[CONSTb], dma=dma_sem("par2"))

        seq = [("w", g) for g in range(20)]
        for jj in range(JC // 2):
            seq.append(("w", 20 + jj))
            if jj >= 1:
                seq.append(("dg", 2 * jj - 1))
            seq.append(("dg", 2 * jj))
        seq.append(("dg", JC - 1))
        seq += [("w", g) for g in range(28, 36)]
        NSEQ = len(seq)
        state = {"cast": 0, "load": 0, "use": 0}
        loaded_q = {}
        cur_q = {}
        total_groups = nt * NSEQ

        widx = {}
        for kind_, g_ in seq:
            if kind_ == "w":
                widx.setdefault(g_, len(widx))

        def issue_load():
            q = state["load"]
            if q >= total_groups:
                return
            state["load"] += 1
            kind, g = seq[q % NSEQ]
            s = q % NRING
            loaded_q[s] = q
            if kind == "w":
                dfr = widx[g] % 2 == 1
                if q < NSEQ:
                    if dfr and state.get("xs_free", False):
                        P.add("sp", lambda e, g=g: e.dma_start(out=XS[:].rearrange("p b d -> p (b d)"), in_=wpack[g]),
                              writes=[XSb], dma=dma_sem("stg"))

                        def castop(g=g, s=s):
                            P.add("act", lambda e: e.activation(out=WR[:, s, :], in_=XS[:].rearrange("p b d -> p (b d)"),
                                                                func=AF.Copy), reads=[XSb], writes=[WRb[s]])
                        pend_cast.append(castop)
                    else:
                        P.add("pool", lambda e, g=g, s=s: e.dma_start(out=WR[:, s, :], in_=wpack[g]),
                              reads=([XSb] if state.get("after_x", False) else []),
                              writes=[WRb[s]], dma=dma_sem(f"wrs{s}"))
                        if not dfr:
                            pend_store[q] = (g, s)
                elif q < 2 * NSEQ and dfr:
                    P.add("pool", lambda e, g=g, s=s: e.dma_start(out=WR[:, s, :], in_=wpack[g]),
                          writes=[WRb[s]], dma=dma_sem(f"wrs{s}"))
                    pend_store[q] = (g, s)
                else:
                    assert WBFb[g].lw, ("bf16 copy not stored yet", g, q)
                    P.add("sp", lambda e, g=g, s=s: e.dma_start(out=WR[:, s, :], in_=wbf[g]),
                          reads=[WBFb[g]], writes=[WRb[s]], dma=dma_sem(f"wr{s}"))
            else:
                assert DGBb[g].lw, "diag group loaded before it was built"
                P.add("sp", lambda e, g=g, s=s: e.dma_start(out=WR[:, s, 0:KP * 128], in_=dgb[g]),
                      reads=[DGBb[g]], writes=[WRb[s]], dma=dma_sem(f"wr{s}"))

        pend_cast = []
        pend_store = {}

        def flush_stores(upto):
            for q0 in sorted(pend_store):
                if q0 < upto:
                    g0, s0 = pend_store.pop(q0)
                    P.add("sp", lambda e, g0=g0, s0=s0: e.dma_start(out=wbf[g0], in_=WR[:, s0, :]),
                          reads=[WRb[s0]], writes=[WBFb[g0]], dma=dma_sem(f"wst{s0}"))

        def next_group(kind, g, defer=False):
            while pend_cast:
                pend_cast.pop(0)()
            if state["load"] >= NSEQ - 8 and not state.get("x1_done", False) and state.get("xs_free", False):
                state["x1_done"] = True
                state["xs_free"] = False
                load_x(1)
            q = state["use"]
            flush_stores(q)
            assert seq[q % NSEQ] == (kind, g), (q, seq[q % NSEQ], kind, g)
            state["use"] += 1
            if not defer:
                issue_load()
            cur_q[q % NRING] = q
            return q % NRING

        def load_x(t):
            if t >= nt:
                return
            tk = t * N
            P.add("sp", lambda e: e.dma_start(out=XS[:, 0:3, :],
                                              in_=xin[tk:tk + 384, :].rearrange("(b p) d -> p b d", p=128)),
                  writes=[XSb], dma=dma_sem("xsa"))
            P.add("sp", lambda e: e.dma_start(out=XS[0:N - 384, 3, :], in_=xin[tk + 384:tk + N, :]),
                  writes=[XSb], dma=dma_sem("xsb"))

        load_x(0)
        state["after_x"] = True
        for _ in range(NRING - 1):
            issue_load()
        state["after_x"] = False

        def build_diag(j):
            ds = 0

            def f(e):
                for k in range(KP):
                    ins = e.tensor_scalar(out=DIAG[:, ds, k, :], in0=identb[:], scalar1=CHP[:, j, 4 + k:5 + k],
                                          scalar2=None, op0=ALU.mult)
                return ins
            P.add("dve", f, reads=[CONSTb], writes=[DIAGb[ds]])
            P.add("sp", lambda e: e.dma_start(out=dgb[j], in_=DIAG[:, ds, 0:KP, :].rearrange("p k m -> p (k m)")),
                  reads=[DIAGb[ds]], writes=[DGBb[j]], dma=dma_sem(f"dgst{ds}"))

        def wsl(s, i):
            return WR[:, s, i * 128:(i + 1) * 128]

        def stats_act(kc, src_ap, src_bufs):
            P.add("act", lambda e: e.activation(out=SQ[:, kc, :], in_=src_ap, func=AF.Square),
                  reads=src_bufs, writes=[SQb[kc]])

        def stats_mm(kc):
            P.add("pe", lambda e: e.matmul(S0, lhsT=ones_dm[:], rhs=SQ[:, kc, :], start=(kc == 0), stop=(kc == KC - 1)),
                  reads=[SQb[kc], CONSTb], writes=[S0b])

        def preload_ln():
            P.add("act", lambda e: e.activation(out=SM[:, 6:7], in_=EPS_R[:, 0:1], func=AF.Ln),
                  reads=[CONSTb], writes=[smb[6]])

        def rms_finish(layer):
            P.add("act", lambda e: e.activation(out=MS[0], in_=S0, func=AF.Ln, bias=EPS_R[:, 0:1], scale=1.0),
                  reads=[S0b, CONSTb], writes=[MS[1]])
            P.add("act", lambda e: e.activation(out=RSTD[0], in_=MS[0], func=AF.Exp, scale=-0.5),
                  reads=[MS[1]], writes=[RSTD[1]])
            for kc in range(KC):
                P.add("dve", lambda e, kc=kc: e.scalar_tensor_tensor(
                    out=HN[:, kc, :], in0=H[:, kc, :], scalar=DMP[:, kc, layer:layer + 1], in1=RSTD[0],
                    op0=ALU.mult, op1=ALU.mult), reads=[Hb[kc], RSTD[1], CONSTb], writes=[HNb[kc]])

        def chk(slot):
            assert loaded_q[slot] == cur_q[slot], (slot, loaded_q[slot], cur_q[slot])

        def proj(slot, wi0, rhs_t, nk, reads):
            chk(slot)
            bank, bb = ring.next()

            def f(e):
                for kc in range(nk):
                    ins = e.matmul(bank[:, 0:N], lhsT=wsl(slot, wi0 + kc), rhs=rhs_t[:, kc, :],
                                   start=(kc == 0), stop=(kc == nk - 1))
                return ins
            P.add("pe", f, reads=[WRb[slot]] + reads, writes=[bb])
            return bank[:, 0:N], bb

        def proj_multi(slot, wi0s, rhs_t, nk, rbufs):
            chk(slot)
            outs = [ring.next() for _ in wi0s]
            for kc in range(nk):
                def f(e, kc=kc):
                    for (bank, bb), wi0 in zip(outs, wi0s):
                        ins = e.matmul(bank[:, 0:N], lhsT=wsl(slot, wi0 + kc), rhs=rhs_t[:, kc, :],
                                       start=(kc == 0), stop=(kc == nk - 1))
                    return ins
                P.add("pe", f, reads=[WRb[slot], rbufs[kc]], writes=[bb for _, bb in outs])
            return [(bank[:, 0:N], bb) for bank, bb in outs]

        def out_proj_m(wg0):
            pend = None
            for m in range(KC):
                if m % 2 == 0:
                    slot = next_group("w", wg0 + m // 2)
                ml = m % 2
                ps, pb = proj(slot, ml * JC, G, JC, Gb)
                if pend is not None:
                    stats_mm(pend)
                P.add("dve", lambda e, m=m, ps=ps: e.tensor_tensor(out=H[:, m, :], in0=H[:, m, :], in1=ps, op=ALU.add),
                      reads=[Hb[m], pb], writes=[Hb[m]])
                stats_act(m, H[:, m, :], [Hb[m]])
                pend = m
            stats_mm(pend)

        def tile(t):
            tok0 = t * N
            pend = None
            for kc in range(KC):
                bank, bb = ring.next()

                def f(e, kc=kc, bank=bank):
                    for b, (o, nb) in enumerate(BLK):
                        ins = e.transpose(out=bank[:, o:o + nb], in_=XS[0:nb, b, kc * 128:(kc + 1) * 128],
                                          identity=identf[0:nb, 0:nb])
                    return ins
                P.add("pe", f, reads=[XSb, CONSTb], writes=[bb])
                if pend is not None:
                    stats_mm(pend)
                P.add("act", lambda e, kc=kc, bank=bank: e.activation(out=H[:, kc, :], in_=bank[:, 0:N], func=AF.Copy),
                      reads=[bb], writes=[Hb[kc]])
                stats_act(kc, bank[:, 0:N], [bb])
                pend = kc
            stats_mm(pend)
            if t > 0:
                load_x(t + 1)
            else:
                state["xs_free"] = True

            rms_finish(0)
            for j in range(JC):
                slot = next_group("w", j)
                if t == 0:
                    build_diag(j)
                if j == 0:
                    (xv_ps, xv_b), (cg_ps, cg_b), (z_ps, z_b), (bg_ps, bg_b) = proj_multi(
                        slot, [2 * KC, 1 * KC, 3 * KC, 0 * KC], HN, KC, HNb)
                else:
                    xv_ps, xv_b = proj(slot, 2 * KC, HN, KC, HNb)
                    cg_ps, cg_b = proj(slot, 1 * KC, HN, KC, HNb)
                    z_ps, z_b = proj(slot, 3 * KC, HN, KC, HNb)
                    bg_ps, bg_b = proj(slot, 0 * KC, HN, KC, HNb)
                xv, xvb = XVr.next()
                sz, szb = SZr.next()
                a, ab = Ar.next()
                v, vb = Vr.next()
                P.add("act", lambda e, xv=xv, xv_ps=xv_ps: e.activation(out=xv, in_=xv_ps, func=AF.Copy),
                      reads=[xv_b], writes=[xvb])
                P.add("act", lambda e, sz=sz, z_ps=z_ps: e.activation(out=sz, in_=z_ps, func=AF.Silu),
                      reads=[z_b], writes=[szb])
                P.add("dve", lambda e, j=j, cg_ps=cg_ps, xv=xv: e.tensor_tensor(
                    out=PC[:, j, 2:N + 2], in0=cg_ps, in1=xv, op=ALU.mult),
                    reads=[cg_b, xvb], writes=[PCbody[j]])
                P.add("dve", lambda e, j=j, a=a: e.tensor_scalar(
                    out=a, in0=PC[:, j, 0:N], scalar1=CHP[:, j, 0:1], scalar2=CHP[:, j, 3:4],
                    op0=ALU.mult, op1=ALU.add), reads=[PCbody[j], PChalo[j], CONSTb], writes=[ab])
                for k in (1, 2):
                    P.add("dve", lambda e, j=j, a=a, k=k: e.scalar_tensor_tensor(
                        out=a, in0=PC[:, j, k:N + k], scalar=CHP[:, j, k:k + 1], in1=a,
                        op0=ALU.mult, op1=ALU.add), reads=[PCbody[j], PChalo[j], ab, CONSTb], writes=[ab])
                P.add("dve", lambda e, v=v, bg_ps=bg_ps, a=a: e.tensor_tensor(out=v, in0=bg_ps, in1=a, op=ALU.mult),
                      reads=[bg_b, ab], writes=[vb])
                P.add("dve", lambda e, j=j, v=v, sz=sz: e.tensor_tensor(out=G[:, j, :], in0=v, in1=sz, op=ALU.mult),
                      reads=[vb, szb], writes=[Gb[j]])
                P.add("pool", lambda e, j=j: e.tensor_copy(out=PC[:, j, 0:2], in_=PC[:, j, N:N + 2]),
                      reads=[PCbody[j]], writes=[PChalo[j]])
            preload_ln()
            out_proj_m(16)

            rms_finish(1)
            pend_stats = []

            def conv_pe(j):
                dslot = next_group("dg", j)
                chk(dslot)
                bank, bb = ring.next()

                def f(e):
                    for k in range(KP):
                        ins = e.matmul(bank[:, 0:N], lhsT=wsl(dslot, k), rhs=U[:, j, k:k + N],
                                       start=(k == 0), stop=(k == KP - 1))
                    return ins
                P.add("pe", f, reads=[Ubody[j], Uhalo[j], WRb[dslot]], writes=[bb])
                return [j, bank[:, 0:N], bb]

            def conv_tap(c, k):
                j, src, srcb, acc, accb = c
                P.add("dve", lambda e: e.scalar_tensor_tensor(
                    out=acc, in0=U[:, j, k:k + N], scalar=CHP[:, j, 4 + k:5 + k], in1=src,
                    op0=ALU.mult, op1=ALU.add), reads=[Ubody[j], Uhalo[j], srcb, CONSTb], writes=[accb])
                c[1], c[2] = acc, accb

            def conv_first(ctxs):
                if KP < 31:
                    for c in ctxs:
                        c.extend(ACCr.next())
                        conv_tap(c, KP)

            def conv_post(ctxs):
                while pend_stats:
                    pend_stats.pop(0)()
                for k in range(KP + 1, 31):
                    for c in ctxs:
                        conv_tap(c, k)
                ctxs = [c[0:3] for c in ctxs]
                for j, src, srcb in ctxs:
                    conv_evac(j, src, srcb)

            def conv_evac(j, src, srcb):
                cbf, cbfb = CBFr.next()
                csq, csqb = CSQr.next()
                P.add("act", lambda e: e.activation(out=PC[:, j, 2:N + 2], in_=src, func=AF.Identity,
                                                    bias=CHP[:, j, 35:36], scale=0.5),
                      reads=[srcb, CONSTb], writes=[PCbody[j]])
                P.add("act", lambda e: e.activation(out=csq, in_=src, func=AF.Square,
                                                    bias=CHP[:, j, 35:36], scale=0.5),
                      reads=[srcb, CONSTb], writes=[csqb])
                P.add("act", lambda e: e.activation(out=cbf, in_=src, func=AF.Identity,
                                                    bias=CHP[:, j, 35:36], scale=0.5),
                      reads=[srcb, CONSTb], writes=[cbfb])

                def g(e):
                    e.matmul(S0, lhsT=ones_ch[:], rhs=cbf, start=(j == 0), stop=(j == JC - 1))
                    return e.matmul(S1, lhsT=ones_ch[:], rhs=csq, start=(j == 0), stop=(j == JC - 1))
                pend_stats.append(lambda: P.add("pe", g, reads=[cbfb, csqb, CONSTb], writes=[S0b, S1b]))
                P.add("pool", lambda e: e.tensor_copy(out=U[:, j, 0:30], in_=U[:, j, N:N + 30]),
                      reads=[Ubody[j]], writes=[Uhalo[j]])

            def phaseA(j, slot, pre=None):
                jl = j % 2
                if pre is not None:
                    (ua_ps, ua_b), (ub_ps, ub_b) = pre
                else:
                    ua_ps, ua_b = proj(slot, (jl * 2 + 0) * KC, HN, KC, HNb)
                    ub_ps, ub_b = proj(slot, (jl * 2 + 1) * KC, HN, KC, HNb)
                sg, sgb = SGr.next()
                P.add("act", lambda e: e.activation(out=sg, in_=ub_ps, func=AF.Tanh, scale=0.5),
                      reads=[ub_b], writes=[sgb])
                P.add("dve", lambda e: e.scalar_tensor_tensor(out=U[:, j, 30:N + 30], in0=sg, scalar=1.0, in1=ua_ps,
                                                              op0=ALU.add, op1=ALU.mult),
                      reads=[ua_b, sgb], writes=[Ubody[j]])

            def zproj_pe(j, slot):
                return (j,) + proj(slot, (j % 4) * KC, HN, KC, HNb)

            def zproj_act(j, z_ps, z_b):
                P.add("act", lambda e: e.activation(out=SZ16[:, j, :], in_=z_ps, func=AF.Silu),
                      reads=[z_b], writes=[SZb[j]])

            def zproj(j, slot):
                zproj_act(*zproj_pe(j, slot))

            pend_post = None
            for jj in range(JC // 2):
                slot = next_group("w", 20 + jj)
                if jj == 0:
                    r4 = proj_multi(slot, [0, KC, 2 * KC, 3 * KC], HN, KC, HNb)
                    phaseA(0, slot, pre=r4[0:2])
                    phaseA(1, slot, pre=r4[2:4])
                else:
                    phaseA(2 * jj, slot)
                    phaseA(2 * jj + 1, slot)
                if pend_post is not None:
                    conv_post(pend_post)
                cs = []
                if jj >= 1:
                    cs.append(conv_pe(2 * jj - 1))
                cs.append(conv_pe(2 * jj))
                conv_first(cs)
                pend_post = cs
            conv_post(pend_post)
            cs = [conv_pe(JC - 1)]
            conv_first(cs)
            preload_ln()
            ZLEAD = 5
            zlead = []
            for j in range(ZLEAD):
                if j % 4 == 0:
                    zslot = next_group("w", 28 + j // 4)
                zlead.append(zproj_pe(j, zslot))
                if j == 2:
                    conv_post(cs)
            while pend_stats:
                pend_stats.pop(0)()
            P.add("dve", lambda e: e.tensor_copy(out=MEAN[0], in_=S0), reads=[S0b], writes=[MEAN[1]])
            P.add("dve", lambda e: e.tensor_tensor(out=M2[0], in0=MEAN[0], in1=MEAN[0], op=ALU.mult),
                  reads=[MEAN[1]], writes=[M2[1]])
            P.add("dve", lambda e: e.scalar_tensor_tensor(out=MS[0], in0=S1, scalar=LN_EPS, in1=M2[0],
                                                          op0=ALU.add, op1=ALU.subtract),
                  reads=[S1b, M2[1]], writes=[MS[1]])
            P.add("act", lambda e: e.activation(out=M2[0], in_=MS[0], func=AF.Ln), reads=[MS[1]], writes=[M2[1]])
            P.add("act", lambda e: e.activation(out=RSTD[0], in_=M2[0], func=AF.Exp, scale=-0.5),
                  reads=[M2[1]], writes=[RSTD[1]])
            for zl in zlead:
                zproj_act(*zl)
            P.add("dve", lambda e: e.tensor_tensor(out=MR[0], in0=MEAN[0], in1=RSTD[0], op=ALU.mult),
                  reads=[MEAN[1], RSTD[1]], writes=[MR[1]])

            def norm_a(j):
                a, ab = Ar.next()
                v, vb = Vr.next()
                xv, xvb = XVr.next()
                P.add("dve", lambda e: e.tensor_tensor(out=a, in0=PC[:, j, 2:N + 2], in1=RSTD[0], op=ALU.mult),
                      reads=[PCbody[j], RSTD[1]], writes=[ab])
                P.add("dve", lambda e: e.tensor_tensor(out=v, in0=a, in1=MR[0], op=ALU.subtract),
                      reads=[ab, MR[1]], writes=[vb])
                P.add("act", lambda e: e.activation(out=xv, in_=v, func=AF.Silu,
                                                    bias=CHP[:, j, 37:38], scale=CHP[:, j, 36:37]),
                      reads=[vb, CONSTb], writes=[xvb])
                return xv, xvb

            def norm_g(j, xv, xvb):
                P.add("dve", lambda e: e.tensor_tensor(out=G[:, j, :], in0=xv, in1=SZ16[:, j, :], op=ALU.mult),
                      reads=[xvb, SZb[j]], writes=[Gb[j]])

            SKEW = 2
            pendn = []
            for j in range(JC):
                jz = j + ZLEAD
                if jz < JC:
                    if jz % 4 == 0:
                        zslot = next_group("w", 28 + jz // 4)
                    zproj(jz, zslot)
                pendn.append((j,) + norm_a(j))
                if len(pendn) > SKEW:
                    norm_g(*pendn.pop(0))
            while pendn:
                norm_g(*pendn.pop(0))

            oslots = [next_group("w", 32, defer=False), next_group("w", 33, defer=True)]
            for sl in oslots:
                chk(sl)
            accs = [ring.next() for _ in range(4)]
            for j in range(JC):
                def f(e, j=j):
                    for m in range(4):
                        ins = e.matmul(accs[m][0][:, 0:N], lhsT=wsl(oslots[m // 2], (m % 2) * JC + j), rhs=G[:, j, :],
                                       start=(j == 0), stop=(j == JC - 1))
                    return ins
                P.add("pe", f, reads=[Gb[j]] + [WRb[sl] for sl in oslots], writes=[ab_ for _, ab_ in accs])
            issue_load()
            for m in range(4):
                P.add("dve", lambda e, m=m: e.tensor_tensor(out=H[:, m, :], in0=H[:, m, :], in1=accs[m][0][:, 0:N], op=ALU.add),
                      reads=[Hb[m], accs[m][1]], writes=[Hb[m]])
            for m in range(4, KC):
                if m % 2 == 0:
                    slot = next_group("w", 32 + m // 2)
                ps, pb = proj(slot, (m % 2) * JC, G, JC, Gb)
                P.add("dve", lambda e, m=m, ps=ps: e.tensor_tensor(out=H[:, m, :], in0=H[:, m, :], in1=ps, op=ALU.add),
                      reads=[Hb[m], pb], writes=[Hb[m]])

            preload_ln()
            dbl = [(Dp[3], [S0b, S1b]), (Dp[2], [bankb[4], bankb[5]])]
            for b, (o, nb) in enumerate(BLK):
                pt, ptb = dbl[b % 2]

                def f(e, o=o, nb=nb, pt=pt):
                    for kc in range(KC):
                        ins = e.transpose(out=pt[0:nb, kc * 128:(kc + 1) * 128], in_=H[:, kc, o:o + nb],
                                          identity=identf[:, :])
                    return ins
                P.add("pe", f, reads=Hb + [CONSTb], writes=ptb)
                ssq, ssqb = SSQr.next()
                ms1, ms1b = MS1r.next()
                rs1, rs1b = RS1r.next()
                osl, osb = OSr.next()
                P.add("act", lambda e, nb=nb, ssq=ssq, pt=pt: e.activation(out=JUNK[0:nb, :], in_=pt[0:nb, :], func=AF.Square,
                                                                         accum_out=ssq[0:nb, :]),
                      reads=ptb, writes=[JUNKb, ssqb])
                P.add("dve", lambda e, nb=nb, ssq=ssq, ms1=ms1: e.tensor_scalar(
                    out=ms1[0:nb, :], in0=ssq[0:nb, :], scalar1=1.0 / D, scalar2=RMS_EPS, op0=ALU.mult, op1=ALU.add),
                    reads=[ssqb], writes=[ms1b])
                P.add("pool", lambda e, nb=nb, ms1=ms1, rs1=rs1: e.tensor_tensor(
                    out=rs1[0:nb, :], in0=ms1[0:nb, :], in1=CST[0:nb, 0:1], op=ALU.pow),
                    reads=[ms1b, CONSTb], writes=[rs1b])
                P.add("dve", lambda e, nb=nb, rs1=rs1, osl=osl, pt=pt: e.scalar_tensor_tensor(
                    out=OS[0:nb, osl, :], in0=pt[0:nb, :], scalar=rs1[0:nb, :], in1=GFIN[0:nb, :],
                    op0=ALU.mult, op1=ALU.mult), reads=ptb + [rs1b, CONSTb], writes=[osb])
                P.add("sp", lambda e, nb=nb, o=o, osl=osl: e.dma_start(out=out[tok0 + o:tok0 + o + nb, :],
                                                                     in_=OS[0:nb, osl, :]),
                      reads=[osb], dma=dma_sem(f"os{osl}"))

        for t in range(nt):
            tile(t)
            if t == 0:
                while pend_cast:
                    pend_cast.pop(0)()
                if not state.get("x1_done", False):
                    state["x1_done"] = True
                    load_x(1)

        engobj = {"pe": nc.tensor, "act": nc.scalar, "dve": nc.vector, "pool": nc.gpsimd, "sp": nc.sync}
        P.emit(engobj, sems)
        for osl in range(2):
            name = f"os{osl}"
            nc.sync.wait_ge(dsem[name], P.dma_cnt[name])
    return nc


def _pack_weights(a_w_in, a_w_out, b_w_in, b_w_out):
    groups = []
    A = a_w_in.reshape(KC, 128, 4, JC, 128)
    for j in range(JC):
        groups.append(A[:, :, :, j, :].transpose(1, 2, 0, 3).reshape(128, GW))
    AO = a_w_out.reshape(JC, 128, KC, 128)
    for mm in range(4):
        groups.append(AO[:, :, 2 * mm:2 * mm + 2, :].transpose(1, 2, 0, 3).reshape(128, GW))
    B = b_w_in.reshape(KC, 128, 3, JC, 128)
    for jj in range(8):
        blk = B[:, :, 0:2, 2 * jj:2 * jj + 2, :]
        groups.append(blk.transpose(1, 3, 2, 0, 4).reshape(128, GW))
    for g in range(4):
        blk = B[:, :, 2, 4 * g:4 * g + 4, :]
        groups.append(blk.transpose(1, 2, 0, 3).reshape(128, GW))
    BO = b_w_out.reshape(JC, 128, KC, 128)
    for mm in range(4):
        groups.append(BO[:, :, 2 * mm:2 * mm + 2, :].transpose(1, 2, 0, 3).reshape(128, GW))
    return np.ascontiguousarray(np.stack(groups, axis=0), dtype=np.float32)


def _prep_inputs(x, meta, norm_g, a_w_in, a_conv_w, a_conv_b, a_w_out, b_w_in, b_conv_w, b_conv_b,
                 b_ln_g, b_ln_b, b_w_out, final_g):
    f = lambda a: np.asarray(a, dtype=np.float32)
    x, meta, norm_g, final_g = f(x), f(meta), f(norm_g), f(final_g)
    wpack = _pack_weights(f(a_w_in)[0], f(a_w_out)[0], f(b_w_in)[0], f(b_w_out)[0])
    par = np.concatenate([f(a_conv_w)[0], f(a_conv_b), f(b_conv_w)[0], f(b_conv_b), f(b_ln_g), f(b_ln_b)], axis=0)
    assert par.shape == (NPAR, DI)
    chanp = np.ascontiguousarray(par.reshape(NPAR, JC, 128).transpose(2, 1, 0).reshape(128, JC * NPAR))
    dmp = np.ascontiguousarray(norm_g.reshape(2, KC, 128).transpose(2, 1, 0).reshape(128, KC * 2))
    gfin = np.ascontiguousarray(np.broadcast_to(final_g[None, :], (128, D)))
    in_maps = []
    for c in range(NCORES):
        b = c // 2
        if c % 2 == 0:
            xin = np.concatenate([meta, x[b, :TOK - NMETA]], axis=0)
        else:
            xin = x[b, SEQ - TOK:]
        in_maps.append({"xin": np.ascontiguousarray(xin), "wpack": wpack, "chanp": chanp, "dmp": dmp, "gfin": gfin})
    return in_maps


def kernel(x, meta, norm_g, a_w_in, a_conv_w, a_conv_b, a_w_out, b_w_in, b_conv_w, b_conv_b,
           b_ln_g, b_ln_b, b_w_out, final_g):
    in_maps = _prep_inputs(x, meta, norm_g, a_w_in, a_conv_w, a_conv_b, a_w_out, b_w_in, b_conv_w, b_conv_b,
                           b_ln_g, b_ln_b, b_w_out, final_g)
    nc = build_nc()
    res = run_bass_kernel_spmd(nc, in_maps, core_ids=list(range(NCORES)))
    outs = [np.asarray(r["out"]) for r in res.results]
    full = np.empty((BATCH, SEQ, D), dtype=np.float32)
    split = TOK - NMETA
    for b in range(BATCH):
        full[b, :split] = outs[2 * b][NMETA:]
        full[b, split:] = outs[2 * b + 1][TOK - (SEQ - split):]
    return full
```
